# Optimizing a Trainium2 kernel written in Bass

```python
import jax, jax.numpy as jnp
from jax import lax
import numpy as np

D_MODEL = 1024
BATCH = 8
SEQ = 2048
DEPTH = 1
DEC_BATCH = 128
DEC_SEQ = 8
PAST_LEN = 16384
PAGE_SIZE = 128

D_MIX = D_MODEL
D_SSD = D_MIX // 2
SSD_HEAD_DIM = 64
N_SSD_HEADS = D_SSD // SSD_HEAD_DIM
SSD_STATE = 128
SSD_GROUPS = 2
CONV_W = 4
CONV_DIM = D_SSD + 2 * SSD_GROUPS * SSD_STATE
SSD_CHUNK = 64
D_GLA = D_MIX - D_SSD
N_GLA_HEADS = 4
GLA_DV = D_GLA // N_GLA_HEADS
GLA_DK = GLA_DV // 2
GLA_RANK = 16
GLA_GATE_NORMALIZER = 16.0
GLA_CHUNK = 64
D_FF = 2816
EPS = 1e-6
SSD_SPLITS = (D_SSD, CONV_DIM, N_SSD_HEADS)
GLA_SPLITS = (N_GLA_HEADS * GLA_DK, N_GLA_HEADS * GLA_DK, D_GLA, D_GLA, GLA_RANK)
IN_COLS = sum(SSD_SPLITS) + sum(GLA_SPLITS)

kernel_name = 'hymba_ssd_gla_macaron_step'


def _split(x, sizes):
    cuts = [int(c) for c in np.cumsum(sizes)[:-1]]
    return jnp.split(x, cuts, axis=-1)


def _rms(x, w):
    xf = x.astype(jnp.float32)
    out = xf * lax.rsqrt(jnp.mean(xf * xf, axis=-1, keepdims=True) + EPS) * w.astype(jnp.float32)
    return out.astype(x.dtype)


def _swiglu(x, w_up, w_down):
    gate, up = jnp.split(x @ w_up, 2, axis=-1)
    return (jax.nn.silu(gate) * up) @ w_down


def _chunk_len(l, c):
    return c if l % c == 0 else l


def _to_chunks(t, lc):
    b, l = t.shape[:2]
    return jnp.moveaxis(t.reshape((b, l // lc, lc) + t.shape[2:]), 1, 0)


def _from_chunks(t):
    t = jnp.moveaxis(t, 0, 1)
    return t.reshape((t.shape[0], t.shape[1] * t.shape[2]) + t.shape[3:])


def _ssd_chunked(xh, dt, a, bh, ch, s0):
    lc = _chunk_len(xh.shape[1], SSD_CHUNK)
    mask = jnp.tril(jnp.ones((lc, lc), dtype=bool))

    def step(s, inp):
        xc, dtc, bc, cc = inp
        acum = jnp.cumsum(dtc * a, axis=1)
        seg = acum[:, :, None, :] - acum[:, None, :, :]
        decay = jnp.exp(jnp.where(mask[None, :, :, None], seg, -jnp.inf))
        xdt = xc * dtc[..., None]
        scores = jnp.einsum('bihn,bjhn->bijh', cc, bc) * decay
        y = (jnp.einsum('bijh,bjhp->bihp', scores, xdt)
             + jnp.einsum('bihn,bhpn->bihp', cc, s) * jnp.exp(acum)[..., None])
        to_end = jnp.exp(acum[:, -1:, :] - acum)
        s_new = (s * jnp.exp(acum[:, -1, :])[:, :, None, None]
                 + jnp.einsum('bjhn,bjhp->bhpn', bc * to_end[..., None], xdt))
        return s_new, y

    s_fin, ys = lax.scan(step, s0, tuple(_to_chunks(t, lc) for t in (xh, dt, bh, ch)))
    return _from_chunks(ys), s_fin


def _gla_chunked(q, k, v, log_a, s0):
    lc = _chunk_len(q.shape[1], GLA_CHUNK)
    mask = jnp.tril(jnp.ones((lc, lc), dtype=bool))

    def step(s, inp):
        qc, kc, vc, lac = inp
        bcum = jnp.cumsum(lac, axis=1)
        seg = bcum[:, :, None] - bcum[:, None, :]
        decay = jnp.exp(jnp.where(mask[None, :, :, None, None], seg, -jnp.inf))
        att = jnp.einsum('bihd,bjhd,bijhd->bijh', qc, kc, decay)
        o = (jnp.einsum('bijh,bjhv->bihv', att, vc)
             + jnp.einsum('bihd,bhdv->bihv', qc * jnp.exp(bcum), s))
        kd = kc * jnp.exp(bcum[:, -1:] - bcum)
        s_new = s * jnp.exp(bcum[:, -1])[..., None] + jnp.einsum('bjhd,bjhv->bhdv', kd, vc)
        return s_new, o

    s_fin, os_ = lax.scan(step, s0, tuple(_to_chunks(t, lc) for t in (q, k, v, log_a)))
    return _from_chunks(os_), s_fin


def _mixer(h, ssd0, conv0, gla0, w_in, conv_w, conv_b, dt_bias, a_log, d_skip, g_ssd,
           w_gk2, b_gk, g_gla, w_out):
    f32 = jnp.float32
    bsz, l, _ = h.shape
    z, xbc, dt, q, k, v, g, gk_lr = _split(h @ w_in, SSD_SPLITS + GLA_SPLITS)
    xpad = jnp.concatenate([conv0.astype(xbc.dtype), xbc], axis=1)
    conv = conv_b
    for i in range(CONV_W):
        conv = conv + xpad[:, i:i + l] * conv_w[i]
    new_conv = xpad[:, l:]
    xs, bm, cm = _split(jax.nn.silu(conv).astype(f32),
                        (D_SSD, SSD_GROUPS * SSD_STATE, SSD_GROUPS * SSD_STATE))
    rep = N_SSD_HEADS // SSD_GROUPS
    xh = xs.reshape(bsz, l, N_SSD_HEADS, SSD_HEAD_DIM)
    bh = jnp.repeat(bm.reshape(bsz, l, SSD_GROUPS, SSD_STATE), rep, axis=2)
    ch = jnp.repeat(cm.reshape(bsz, l, SSD_GROUPS, SSD_STATE), rep, axis=2)
    dtp = jax.nn.softplus(dt.astype(f32) + dt_bias.astype(f32))
    a = -jnp.exp(a_log.astype(f32))
    y, ssd_new = _ssd_chunked(xh, dtp, a, bh, ch, ssd0.astype(f32))
    y = y + d_skip.astype(f32)[:, None] * xh
    y = y.reshape(bsz, l, D_SSD) * jax.nn.silu(z.astype(f32))
    gsz = D_SSD // SSD_GROUPS
    y = _rms(y.reshape(bsz, l, SSD_GROUPS, gsz), g_ssd.reshape(SSD_GROUPS, gsz)).reshape(bsz, l, D_SSD)
    qh = q.astype(f32).reshape(bsz, l, N_GLA_HEADS, GLA_DK) * GLA_DK ** -0.5
    kh = k.astype(f32).reshape(bsz, l, N_GLA_HEADS, GLA_DK)
    vh = v.astype(f32).reshape(bsz, l, N_GLA_HEADS, GLA_DV)
    log_a = jax.nn.log_sigmoid((gk_lr @ w_gk2 + b_gk).astype(f32)) / GLA_GATE_NORMALIZER
    o, gla_new = _gla_chunked(qh, kh, vh, log_a.reshape(bsz, l, N_GLA_HEADS, GLA_DK), gla0.astype(f32))
    o = _rms(o, g_gla).reshape(bsz, l, D_GLA) * jax.nn.silu(g.astype(f32))
    mixed = jnp.concatenate([y, o], axis=-1).astype(h.dtype)
    return mixed @ w_out, (ssd_new.astype(ssd0.dtype), new_conv.astype(conv0.dtype), gla_new.astype(gla0.dtype))


def _layer(x, ssd0, conv0, gla0, g_ffn1, w_ffn1_in, w_ffn1_out, g_mix, w_in, conv_w, conv_b,
           dt_bias, a_log, d_skip, g_ssd_norm, w_gk2, b_gk, g_gla_norm, w_out, g_ffn2,
           w_ffn2_in, w_ffn2_out):
    x = x + 0.5 * _swiglu(_rms(x, g_ffn1), w_ffn1_in, w_ffn1_out)
    mix, new_state = _mixer(_rms(x, g_mix), ssd0, conv0, gla0, w_in, conv_w, conv_b, dt_bias, a_log,
                            d_skip, g_ssd_norm, w_gk2, b_gk, g_gla_norm, w_out)
    x = x + mix
    x = x + 0.5 * _swiglu(_rms(x, g_ffn2), w_ffn2_in, w_ffn2_out)
    return x, new_state


def setup_inputs(seed: int = 0) -> dict:
    key = jax.random.key(seed)
    ks = jax.random.split(key, 32)
    nrm = jax.random.normal
    f32 = jnp.float32
    x_prompt = nrm(ks[0], (BATCH, SEQ, D_MODEL), f32)
    x_sample = nrm(ks[1], (DEC_BATCH, DEC_SEQ, D_MODEL), f32)
    state_ssd = 0.3 * nrm(ks[2], (DEPTH, DEC_BATCH, N_SSD_HEADS, SSD_HEAD_DIM, SSD_STATE), f32)
    state_conv = nrm(ks[3], (DEPTH, DEC_BATCH, CONV_W - 1, CONV_DIM), f32)
    state_gla = 0.3 * nrm(ks[4], (DEPTH, DEC_BATCH, N_GLA_HEADS, GLA_DK, GLA_DV), f32)
    g_ffn1 = 1.0 + 0.05 * nrm(ks[5], (DEPTH, D_MODEL), f32)
    w_ffn1_in = nrm(ks[6], (DEPTH, D_MODEL, 2 * D_FF), f32) * D_MODEL ** -0.5
    w_ffn1_out = nrm(ks[7], (DEPTH, D_FF, D_MODEL), f32) * D_FF ** -0.5
    g_mix = 1.0 + 0.05 * nrm(ks[8], (DEPTH, D_MODEL), f32)
    w_in = nrm(ks[9], (DEPTH, D_MODEL, IN_COLS), f32) * D_MODEL ** -0.5
    conv_w = nrm(ks[10], (DEPTH, CONV_W, CONV_DIM), f32) * CONV_W ** -0.5
    conv_b = 0.02 * nrm(ks[11], (DEPTH, CONV_DIM), f32)
    dt0 = jnp.exp(jax.random.uniform(ks[12], (DEPTH, N_SSD_HEADS), f32, np.log(1e-3), np.log(1e-1)))
    dt_bias = dt0 + jnp.log(-jnp.expm1(-dt0))
    a_log = jnp.log(jax.random.uniform(ks[13], (DEPTH, N_SSD_HEADS), f32, 1.0, 16.0))
    d_skip = 1.0 + 0.1 * nrm(ks[14], (DEPTH, N_SSD_HEADS), f32)
    g_ssd_norm = 1.0 + 0.05 * nrm(ks[15], (DEPTH, D_SSD), f32)
    w_gk2 = nrm(ks[16], (DEPTH, GLA_RANK, N_GLA_HEADS * GLA_DK), f32) * GLA_RANK ** -0.5
    b_gk = 0.1 * nrm(ks[17], (DEPTH, N_GLA_HEADS * GLA_DK), f32)
    g_gla_norm = 1.0 + 0.05 * nrm(ks[18], (DEPTH, GLA_DV), f32)
    w_out = nrm(ks[19], (DEPTH, D_MIX, D_MODEL), f32) * D_MIX ** -0.5
    g_ffn2 = 1.0 + 0.05 * nrm(ks[20], (DEPTH, D_MODEL), f32)
    w_ffn2_in = nrm(ks[21], (DEPTH, D_MODEL, 2 * D_FF), f32) * D_MODEL ** -0.5
    w_ffn2_out = nrm(ks[22], (DEPTH, D_FF, D_MODEL), f32) * D_FF ** -0.5
    g_final = 1.0 + 0.05 * nrm(ks[23], (D_MODEL,), f32)
    return {'x_prompt': x_prompt, 'x_sample': x_sample, 'state_ssd': state_ssd,
            'state_conv': state_conv, 'state_gla': state_gla, 'g_ffn1': g_ffn1,
            'w_ffn1_in': w_ffn1_in, 'w_ffn1_out': w_ffn1_out, 'g_mix': g_mix, 'w_in': w_in,
            'conv_w': conv_w, 'conv_b': conv_b, 'dt_bias': dt_bias, 'a_log': a_log,
            'd_skip': d_skip, 'g_ssd_norm': g_ssd_norm, 'w_gk2': w_gk2, 'b_gk': b_gk,
            'g_gla_norm': g_gla_norm, 'w_out': w_out, 'g_ffn2': g_ffn2,
            'w_ffn2_in': w_ffn2_in, 'w_ffn2_out': w_ffn2_out, 'g_final': g_final}


def reference(x_prompt, x_sample, state_ssd, state_conv, state_gla, g_ffn1, w_ffn1_in, w_ffn1_out,
              g_mix, w_in, conv_w, conv_b, dt_bias, a_log, d_skip, g_ssd_norm, w_gk2, b_gk,
              g_gla_norm, w_out, g_ffn2, w_ffn2_in, w_ffn2_out, g_final):
    bp = x_prompt.shape[0]
    xp, xs = x_prompt, x_sample
    p_ssd, p_conv, p_gla, s_ssd, s_conv, s_gla = [], [], [], [], [], []
    for i in range(DEPTH):
        lw = (g_ffn1[i], w_ffn1_in[i], w_ffn1_out[i], g_mix[i], w_in[i], conv_w[i], conv_b[i],
              dt_bias[i], a_log[i], d_skip[i], g_ssd_norm[i], w_gk2[i], b_gk[i], g_gla_norm[i],
              w_out[i], g_ffn2[i], w_ffn2_in[i], w_ffn2_out[i])
        z_ssd = jnp.zeros((bp,) + state_ssd.shape[2:], state_ssd.dtype)
        z_conv = jnp.zeros((bp,) + state_conv.shape[2:], state_conv.dtype)
        z_gla = jnp.zeros((bp,) + state_gla.shape[2:], state_gla.dtype)
        xp, (a1, a2, a3) = _layer(xp, z_ssd, z_conv, z_gla, *lw)
        xs, (b1, b2, b3) = _layer(xs, state_ssd[i], state_conv[i], state_gla[i], *lw)
        p_ssd.append(a1); p_conv.append(a2); p_gla.append(a3)
        s_ssd.append(b1); s_conv.append(b2); s_gla.append(b3)
    y_prompt = _rms(xp, g_final)
    y_sample = _rms(xs, g_final)
    return (y_prompt, y_sample, jnp.stack(p_ssd), jnp.stack(p_conv), jnp.stack(p_gla),
            jnp.stack(s_ssd), jnp.stack(s_conv), jnp.stack(s_gla))
```

```python
import numpy as np
import ml_dtypes
import concourse.bass as bass
import concourse.mybir as mybir
from concourse.bass_utils import run_bass_kernel_spmd

F32 = mybir.dt.float32
BF16 = mybir.dt.bfloat16
AF = mybir.ActivationFunctionType
ALU = mybir.AluOpType

NCORES = 8
D = 1024
KC = 8
DFF = 2816
MH = 22
INC = 3096
EPS = 1e-6
SBUF_F32 = 53100
EPOCH = 8000
DBG_STOP = 99
DBG_B = 3
N_WARM = 0
PROBE_SKIP_S = False
FINE_INTERLEAVE = True
IL_RATIO = 3
DBG_SKIP_SAMPLE = False


def _esize(dt):
    return mybir.dt.size(dt)


class DSem:
    def __init__(self, idx):
        self.idx = idx
        self.count = 0
        self.h = None


class Op:
    __slots__ = ("eng", "fn", "waits", "signal", "seq", "dsem", "cnt")


class Prog:
    ENGS = ("pe", "act", "dve", "pool", "sp")

    def __init__(self):
        self.ops = {e: [] for e in self.ENGS}
        self.known = {e: {} for e in self.ENGS}
        self.track = {}
        self.dsems = []

    def dsem(self):
        d = DSem(len(self.dsems))
        self.dsems.append(d)
        return d

    @staticmethod
    def _region(ap):
        pat = ap.ap
        es = _esize(ap.dtype)
        pstep = pat[0][0]
        off = ap.offset
        if pstep == 0:
            p0, f0 = 0, off
        else:
            p0, f0 = off // pstep, off % pstep
        ext = 1
        for (s, c) in pat[1:]:
            ext += (c - 1) * abs(s)
        return (p0, p0 + pat[0][1], f0 * es, (f0 + ext) * es)

    def _deps(self, ap, is_write, me, deps):
        name = ap.name
        if name.startswith("ps"):
            reg = (0, 128, 0, 2048)
            is_write = True
        else:
            reg = self._region(ap)
        lst = self.track.get(name, [])
        keep = []
        exact = None
        for ent in lst:
            if ent[0] < reg[1] and reg[0] < ent[1] and ent[2] < reg[3] and reg[2] < ent[3]:
                if ent[4] is not None:
                    deps.append((ent[4], "WAW" if is_write else "RAW"))
                if is_write:
                    for ev in ent[5].values():
                        deps.append((ev, "WAR"))
                    if reg[0] <= ent[0] and ent[1] <= reg[1] and reg[2] <= ent[2] and ent[3] <= reg[3]:
                        continue
                elif (ent[0], ent[1], ent[2], ent[3]) == reg:
                    exact = ent
            keep.append(ent)
        if is_write:
            keep.append([reg[0], reg[1], reg[2], reg[3], me, {}])
        else:
            if exact is None:
                exact = [reg[0], reg[1], reg[2], reg[3], None, {}]
                keep.append(exact)
            exact[5][me[0]] = me
        self.track[name] = keep

    def add(self, eng, fn, reads=(), writes=(), dsem=None):
        lst = self.ops[eng]
        op = Op()
        op.eng = eng
        op.fn = fn
        op.signal = False
        op.dsem = dsem
        op.cnt = 0
        op.seq = len(lst) + 1
        if dsem is not None:
            dsem.count += 16
            me = (dsem, dsem.count)
        else:
            me = (eng, op.seq)
        deps = []
        for ap in reads:
            if ap.name in TRACKED:
                self._deps(ap, False, me, deps)
        for ap in writes:
            if ap.name in TRACKED:
                self._deps(ap, True, me, deps)
        need = {}
        for (ev, kind) in deps:
            key, seq = ev
            if ev == me:
                continue
            if key == eng and eng == "pe":
                continue
            if need.get(key, 0) < seq:
                need[key] = seq
        kn = self.known[eng]
        waits = []
        for key, seq in need.items():
            if kn.get(key, 0) >= seq:
                continue
            kn[key] = seq
            waits.append((key, seq))
            if isinstance(key, DSem) and seq != key.count - (16 if key is dsem else 0):
                raise AssertionError("partial DMA-semaphore wait (%d of %d) on dsem %d" % (seq, key.count, key.idx))
            if not isinstance(key, DSem):
                self.ops[key][seq - 1].signal = True
        op.waits = waits
        lst.append(op)
        return op

    def wait_all(self, eng, dsems):
        op = Op()
        op.eng = eng
        op.fn = None
        op.signal = False
        op.dsem = None
        op.cnt = 0
        op.seq = len(self.ops[eng]) + 1
        op.waits = [(d, d.count) for d in dsems if d.count > 0]
        self.ops[eng].append(op)

    def finalize_counts(self):
        self.nsem = {}
        for e in self.ENGS:
            c = 0
            for op in self.ops[e]:
                if op.signal:
                    c += 1
                    op.cnt = c
            self.nsem[e] = max(1, (c + EPOCH - 1) // EPOCH)

    def emit_engine(self, e_name, e, esem):
        for op in self.ops[e_name]:
            for (key, seq) in op.waits:
                if isinstance(key, DSem):
                    e.wait_ge(key.h, seq)
                else:
                    p = self.ops[key][seq - 1]
                    c = p.cnt
                    e.wait_ge(esem[key][(c - 1) // EPOCH], (c - 1) % EPOCH + 1)
            if op.fn is None:
                continue
            ins = op.fn(e)
            if op.dsem is not None:
                ins.then_inc(op.dsem.h, 16)
            elif op.signal:
                c = op.cnt
                ins.then_inc(esem[e_name][(c - 1) // EPOCH], 1)


TRACKED = {"S"} | {"ps%d" % i for i in range(8)}


class Mem:
    def __init__(self, S):
        self.S = S
        self.top = 0
        self.limit = SBUF_F32 * 4

    def mark(self):
        return self.top

    def reset(self, m):
        self.top = m

    def _alloc(self, nbytes):
        off = self.top
        self.top += (nbytes + 31) // 32 * 32
        assert self.top <= self.limit, ("SBUF overflow", self.top, self.limit)
        return off

    def f32(self, *shape):
        n = int(np.prod(shape))
        off = self._alloc(n * 4)
        v = self.S[:, off // 4: off // 4 + n]
        return self._shape(v, shape)

    def bf16(self, *shape):
        n = int(np.prod(shape))
        n2 = (n + 1) // 2
        off = self._alloc(n2 * 4)
        v = self.S[:, off // 4: off // 4 + n2].bitcast(BF16)
        if n2 * 2 != n:
            v = v[:, 0:n]
        return self._shape(v, shape)

    @staticmethod
    def _shape(v, shape):
        if len(shape) == 1:
            return v
        if len(shape) == 2:
            return v.rearrange("p (a b) -> p a b", a=shape[0])
        if len(shape) == 3:
            return v.rearrange("p (a b c) -> p a b c", a=shape[0], b=shape[1])
        raise ValueError(shape)


def bc(ap, shape):
    return ap.to_broadcast(list(shape))


C_ID, C_ONES, C_TRI, C_U, C_TRIB, C_UB, C_ONESB = [i * 128 for i in range(7)]
C_SEQ = 7 * 128
C_EPS = C_SEQ + 16
NCF = C_EPS + 8
PV_G1, PV_GM, PV_G2, PV_GF = 0, 8, 16, 24
PV_CW = 32
PV_CB = 64
PV_DTB = 72
PV_ALOG = 80
PV_DSK = 88
PV_GSSD = 96
PV_GGLA = 608
PV_BGK = 736
NPV = 992


def make_consts():
    c = np.zeros((128, NCF), np.float32)
    k = np.arange(128)[:, None]
    i = np.arange(128)[None, :]
    same = (k // 8) == (i // 8)
    c[:, C_ID:C_ID + 128] = (k == i)
    c[:, C_ONES:C_ONES + 128] = 1.0
    c[:, C_TRI:C_TRI + 128] = (k <= i)
    c[:, C_U:C_U + 128] = (k > i)
    c[:, C_TRIB:C_TRIB + 128] = (k <= i) & same
    c[:, C_UB:C_UB + 128] = (k > i) & same
    c[:, C_ONESB:C_ONESB + 128] = same
    c[:, C_SEQ:C_SEQ + 16] = (k // 8) == np.arange(16)[None, :]
    c[:, C_EPS] = EPS
    return c


def make_pvec(inp):
    pv = np.zeros((128, NPV), np.float32)

    def fm(v):
        return np.ascontiguousarray(np.asarray(v, np.float32).reshape(8, 128).T)

    pv[:, PV_G1:PV_G1 + 8] = fm(inp["g_ffn1"][0])
    pv[:, PV_GM:PV_GM + 8] = fm(inp["g_mix"][0])
    pv[:, PV_G2:PV_G2 + 8] = fm(inp["g_ffn2"][0])
    pv[:, PV_GF:PV_GF + 8] = fm(inp["g_final"])
    cw = np.asarray(inp["conv_w"][0], np.float32)
    for t in range(4):
        pv[:, PV_CW + t * 8:PV_CW + t * 8 + 8] = fm(cw[t])
    pv[:, PV_CB:PV_CB + 8] = fm(inp["conv_b"][0])
    pv[:, PV_DTB:PV_DTB + 8] = np.asarray(inp["dt_bias"][0], np.float32)[None, :]
    pv[:, PV_ALOG:PV_ALOG + 8] = np.asarray(inp["a_log"][0], np.float32)[None, :]
    pv[:, PV_DSK:PV_DSK + 8] = np.asarray(inp["d_skip"][0], np.float32)[None, :]
    pv[:, PV_GSSD:PV_GSSD + 512] = np.asarray(inp["g_ssd_norm"][0], np.float32)[None, :]
    pv[:, PV_GGLA:PV_GGLA + 128] = np.asarray(inp["g_gla_norm"][0], np.float32)[None, :]
    pv[:, PV_BGK:PV_BGK + 256] = np.asarray(inp["b_gk"][0], np.float32)[None, :]
    return pv


def build(PSEQ=2048, do_ffn1=True, do_mixer=True, do_ffn2=True, dbg=None):
    NPC = PSEQ // 128
    NCH = NPC + 1
    NT = NCH * 128
    nc = bass.Bass("TRN2", target_bir_lowering=False)

    def din(name, shape, dt=F32):
        return nc.dram_tensor(name, list(shape), dt, kind="ExternalInput").ap()

    def dout(name, shape, dt=F32):
        return nc.dram_tensor(name, list(shape), dt, kind="ExternalOutput").ap()

    xT_d = din("xT", [D, NT])
    w1i_d = din("w1i", [D, 2 * DFF])
    w1o_d = din("w1o", [DFF, D])
    wi_d = din("wi", [D, INC])
    wo_d = din("wo", [D, D])
    w2i_d = din("w2i", [D, 2 * DFF])
    w2o_d = din("w2o", [DFF, D])
    consts_d = din("consts", [128, NCF])
    pvec_d = din("pvec", [128, NPV])
    wgk2_d = din("wgk2", [16, 256])
    sssd_d = din("st_ssd", [16, 8, 64, 128])
    sconv_d = din("st_conv", [16, 3, 1024])
    sgla_d = din("st_gla", [16, 4, 64, 128])

    yT_d = dout("yT", [D, NT])
    ossd_p_d = dout("o_ssd_p", [8, 64, 128])
    oconv_p_d = dout("o_conv_p", [3, 1024])
    ogla_p_d = dout("o_gla_p", [4, 64, 128])
    ossd_s_d = dout("o_ssd_s", [16, 8, 64, 128])
    oconv_s_d = dout("o_conv_s", [16, 3, 1024])
    ogla_s_d = dout("o_gla_s", [16, 4, 64, 128])
    dbg_d = {}
    if dbg:
        for k, shp in dbg.items():
            dbg_d[k] = dout("dbg_" + k, shp)

    P = Prog()
    from contextlib import ExitStack
    es = ExitStack()
    with es:
        S = es.enter_context(nc.sbuf_tensor("S", [128, SBUF_F32], F32))
        PS = [es.enter_context(nc.psum_tensor("ps%d" % i, [128, 512], F32)) for i in range(8)]
        M = Mem(S)

        def mm(out, lhsT, rhs, start=True, stop=True):
            P.add("pe", lambda e: e.matmul(out, lhsT=lhsT, rhs=rhs, start=start, stop=stop),
                  reads=[lhsT, rhs], writes=[out])

        def tr(out, in_, ident):
            P.add("pe", lambda e: e.transpose(out, in_, ident), reads=[in_, ident], writes=[out])

        def act(out, in_, func, bias=None, scale=None, accum_out=None, eng="act"):
            kw = {}
            rd = [in_]
            wr = [out]
            if bias is not None:
                kw["bias"] = bias
                if not isinstance(bias, float):
                    rd.append(bias)
            if scale is not None:
                kw["scale"] = scale
                if not isinstance(scale, float):
                    rd.append(scale)
            if accum_out is not None:
                kw["accum_out"] = accum_out
                wr.append(accum_out)
            P.add(eng, lambda e: e.activation(out=out, in_=in_, func=func, **kw), reads=rd, writes=wr)

        def tt(out, in0, in1, op, eng="dve"):
            P.add(eng, lambda e: e.tensor_tensor(out=out, in0=in0, in1=in1, op=op), reads=[in0, in1], writes=[out])

        def ts(out, in0, s1, s2, op0, op1=None, eng="dve"):
            rd = [in0] + [s for s in (s1, s2) if s is not None and not isinstance(s, float)]
            if op1 is None:
                P.add(eng, lambda e: e.tensor_scalar(out=out, in0=in0, scalar1=s1, scalar2=None, op0=op0),
                      reads=rd, writes=[out])
            else:
                P.add(eng, lambda e: e.tensor_scalar(out=out, in0=in0, scalar1=s1, scalar2=s2, op0=op0, op1=op1),
                      reads=rd, writes=[out])

        def stt(out, in0, scalar, in1, op0, op1, eng="dve"):
            rd = [in0, in1] + ([] if isinstance(scalar, float) else [scalar])
            P.add(eng, lambda e: e.scalar_tensor_tensor(out=out, in0=in0, scalar=scalar, in1=in1, op0=op0, op1=op1),
                  reads=rd, writes=[out])

        def cp(out, in_, eng="dve"):
            if eng == "act":
                P.add("act", lambda e: e.copy(out=out, in_=in_), reads=[in_], writes=[out])
            else:
                P.add(eng, lambda e: e.tensor_copy(out=out, in_=in_), reads=[in_], writes=[out])

        def memset(ap, val, eng="dve"):
            P.add(eng, lambda e: e.memset(ap, val), writes=[ap])

        def recip(out, in_):
            P.add("dve", lambda e: e.reciprocal(out=out, in_=in_), reads=[in_], writes=[out])

        def dma(q, out, in_, ds):
            P.add(q, lambda e: e.dma_start(out=out, in_=in_), reads=[in_], writes=[out], dsem=ds)

        xT = M.f32(KC, NT)
        ARENA_E = KC * INC + KC * D
        arena = M.bf16(ARENA_E)
        cst = M.f32(NCF)
        pv = M.f32(NPV)
        wgk2 = M.f32(256)
        identb = M.bf16(128)
        onesb = M.bf16(128)
        base_mark = M.mark()

        ident = cst[:, C_ID:C_ID + 128]
        ones_f = cst[:, C_ONES:C_ONES + 128]
        epsc = cst[:, C_EPS:C_EPS + 1]

        dma("sp", cst, consts_d, P.dsem())
        dma("sp", pv, pvec_d, P.dsem())
        dma("sp", wgk2[0:16, :], wgk2_d, P.dsem())
        for t0 in range(0, NT, 512):
            n = min(512, NT - t0)
            for kc in range(KC):
                dma("sp", xT[:, kc, t0:t0 + n], xT_d[kc * 128:(kc + 1) * 128, t0:t0 + n], P.dsem())
        cp(identb, ident)
        cp(onesb, ones_f)

        store_sems = []

        def rms_rstd(tok0, n, sq, rstd, psum_ap):
            for kc in range(KC):
                act(sq[:, kc % 2, 0:n], xT[:, kc, tok0:tok0 + n], AF.Square)
                mm(psum_ap, onesb, sq[:, kc % 2, 0:n], start=(kc == 0), stop=(kc == KC - 1))
            act(rstd[:, 0:n], psum_ap, AF.Ln, bias=epsc, scale=1.0 / D)
            act(rstd[:, 0:n], rstd[:, 0:n], AF.Exp, scale=-0.5)

        def ffn(w_in_d, w_out_d, pv_g, tag, on_done=None):
            mk = M.mark()
            TG = 512
            ntg = (NT + TG - 1) // TG
            base = -(-NT // (ntg * 64)) * 64
            sizes = [base] * ntg
            sizes[-1] = NT - base * (ntg - 1)
            if sizes[-1] > TG:
                sizes = [min(TG, NT - t0) for t0 in range(0, NT, TG)]
            tgs = []
            t0_ = 0
            for n_ in sizes:
                tgs.append((t0_, n_))
                t0_ += n_
            xn = M.bf16(KC, NT)
            sq = M.bf16(2, TG)
            rstd = M.f32(TG)
            GMAX = 5
            groups = [2, 5, 5, 5, 5]
            assert sum(groups) == MH
            gT = [M.bf16(GMAX, TG) for _ in range(2)]
            sil = [M.f32(TG) for _ in range(2)]
            SLOT_E = ARENA_E // 2
            assert KC * 2 * GMAX * 128 + GMAX * D <= SLOT_E
            slot_sem = [[P.dsem() for _ in range(3)] for _ in range(2)]

            def slot_views(s, G):
                base = s * SLOT_E
                wi = arena[:, base: base + KC * 2 * G * 128].rearrange("p (k c) -> p k c", k=KC)
                wo = arena[:, base + KC * 2 * GMAX * 128: base + KC * 2 * GMAX * 128 + G * D].rearrange(
                    "p (m c) -> p m c", m=G)
                return wi, wo

            def load_group(gi, m0, G):
                s = gi % 2
                wi, wo = slot_views(s, G)
                src = w_in_d.rearrange("(k p) c -> p k c", p=128)
                dma("pool", wi[:, :, 0:G * 128], src[:, :, m0 * 128:(m0 + G) * 128], slot_sem[s][0])
                dma("pool", wi[:, :, G * 128:2 * G * 128], src[:, :, DFF + m0 * 128: DFF + (m0 + G) * 128], slot_sem[s][1])
                srco = w_out_d[m0 * 128:(m0 + G) * 128, :].rearrange("(m p) c -> p m c", p=128)
                dma("pool", wo, srco, slot_sem[s][2])

            def norm_tg(ti):
                t0, n = tgs[ti]
                rms_rstd(t0, n, sq, rstd, PS[6][:, 0:n])
                for kc in range(KC):
                    stt(xn[:, kc, t0:t0 + n], xT[:, kc, t0:t0 + n], pv[:, pv_g + kc: pv_g + kc + 1],
                        rstd[:, 0:n], ALU.mult, ALU.mult)

            m0s = np.cumsum([0] + groups)
            load_group(0, int(m0s[0]), groups[0])
            load_group(1, int(m0s[1]), groups[1])
            norm_tg(0)

            state = {"ab": 0, "ob": 0, "gb": 0}

            def stageA(gi, G, t0, n):
                wi, wo = slot_views(gi % 2, G)
                gt = gT[state["gb"] % 2]
                for m in range(G):
                    ab = state["ab"] % 2
                    state["ab"] += 1
                    pg = PS[0 + 2 * ab][:, 0:n]
                    pu = PS[1 + 2 * ab][:, 0:n]
                    for kc in range(KC):
                        mm(pg, wi[:, kc, m * 128:(m + 1) * 128], xn[:, kc, t0:t0 + n], start=(kc == 0), stop=(kc == KC - 1))
                    for kc in range(KC):
                        mm(pu, wi[:, kc, (G + m) * 128:(G + m + 1) * 128], xn[:, kc, t0:t0 + n], start=(kc == 0), stop=(kc == KC - 1))
                    sl = sil[ab]
                    act(sl[:, 0:n], pg, AF.Silu)
                    tt(gt[:, m, 0:n], sl[:, 0:n], pu, ALU.mult)
                return gt

            def stageB(gi, G, t0, n, gt):
                wi, wo = slot_views(gi % 2, G)
                for dc in range(KC):
                    ob = state["ob"] % 2
                    state["ob"] += 1
                    po = PS[4 + ob][:, 0:n]
                    for m in range(G):
                        mm(po, wo[:, m, dc * 128:(dc + 1) * 128], gt[:, m, 0:n], start=(m == 0), stop=(m == G - 1))
                    stt(xT[:, dc, t0:t0 + n], po, 0.5, xT[:, dc, t0:t0 + n], ALU.mult, ALU.add)
                if on_done is not None and gi == len(groups) - 1:
                    on_done(t0, n)

            pend = None
            for gi, G in enumerate(groups):
                for ti, (t0, n) in enumerate(tgs):
                    if gi == 0 and ti + 1 < len(tgs):
                        norm_tg(ti + 1)
                    gt = stageA(gi, G, t0, n)
                    state["gb"] += 1
                    if pend is not None:
                        stageB(*pend)
                    pend = (gi, G, t0, n, gt)
                    if ti == 0 and gi >= 1 and gi + 1 < len(groups):
                        load_group(gi + 1, int(m0s[gi + 1]), groups[gi + 1])
            stageB(*pend)
            M.reset(mk)

        if do_ffn1:
            ffn(w1i_d, w1o_d, PV_G1, "f1")


        def dbg_dump(name, ap):
            if name in dbg_d:
                ds = P.dsem()
                store_sems.append(ds)
                dma("sp", dbg_d[name], ap, ds)

        def mixer():
            mk = M.mark()
            win = arena[:, 0:KC * INC].rearrange("p (k c) -> p k c", k=KC)
            wout = arena[:, KC * INC:KC * INC + KC * D].rearrange("p (k c) -> p k c", k=KC)
            src = wi_d.rearrange("(k p) c -> p k c", p=128)
            for kc in range(KC):
                dma("pool", win[:, kc, :], src[:, kc, :], P.dsem())
            dma("pool", wout, wo_d.rearrange("(k p) c -> p k c", p=128), P.dsem())

            TRI = cst[:, C_TRI:C_TRI + 128]
            U = cst[:, C_U:C_U + 128]
            TRIB = cst[:, C_TRIB:C_TRIB + 128]
            UB = cst[:, C_UB:C_UB + 128]
            ONESB = cst[:, C_ONESB:C_ONESB + 128]
            SEQ = cst[:, C_SEQ:C_SEQ + 16]
            dtb_b = pv[:, PV_DTB:PV_DTB + 8]
            dsk_b = pv[:, PV_DSK:PV_DSK + 8]
            gssd_b = pv[:, PV_GSSD:PV_GSSD + 512]
            ggla_b = pv[:, PV_GGLA:PV_GGLA + 128]
            bgk_b = pv[:, PV_BGK:PV_BGK + 256]

            sq = M.bf16(2, 128)
            rstd = M.f32(128)
            rstd_bufs = [rstd, M.f32(128)]
            rstd_ready = {}
            pad = M.f32(KC, 176)
            acc = [M.f32(128) for _ in range(2)]
            gkT = M.f32(128)
            mT2 = M.bf16(2, KC, 128)
            mixedT = mT2[:, 0]
            hT = mT2[:, 1]
            a_b = M.f32(8)
            dtr = M.f32(8)
            E3 = M.f32(24)
            ssq2 = M.f32(2)
            rs2 = M.f32(2)
            ssq4 = M.f32(4)
            rs4 = M.f32(4)
            t12 = M.f32(1024)
            R = t12
            GTm = M.f32(2, 128)
            sc_raw = M.f32(512)
            scT = sc_raw.bitcast(BF16).rearrange("p (a b) -> p a b", a=8)
            xstok = M.bf16(512)
            xdt = M.bf16(512)
            xdtw = M.bf16(512)
            Btok = M.bf16(256)
            t1 = t12[:, 0:512]
            t2 = t12[:, 512:1024]
            ST = M.f32(512)
            STb = M.bf16(512)
            eq = M.f32(4, 128)
            ek = M.f32(4, 128)
            qtil = M.bf16(4, 128)
            ktil = M.bf16(4, 128)
            esuf = M.f32(256)
            xg = esuf
            kd = M.bf16(256)
            attT = M.bf16(4, 128)
            mtok = M.bf16(1024)
            decT = t12[:, 0:512].bitcast(BF16)
            SG = M.f32(4, 128)
            SGb = M.bf16(4, 128)
            ostage = sc_raw.rearrange("p (q n) -> p q n", q=4)
            cvo = t12

            def alloc_set():
                return (M.bf16(KC, 128), M.f32(4, 128), M.f32(4, 128), M.f32(512), M.f32(512), M.bf16(512), M.f32(256),
                        M.f32(8), M.f32(8), M.f32(256))

            setA = alloc_set()
            PS1b = PS[1][:].bitcast(BF16)
            PS6b = PS[6][:].bitcast(BF16)
            PS2b = PS[2][:].bitcast(BF16)

            act(a_b, pv[:, PV_ALOG:PV_ALOG + 8], AF.Exp)
            ts(a_b, a_b, -1.0, None, ALU.mult)
            memset(pad, 0.0)

            fm_i = [0]

            fm_banks = [[0, 1]]

            def fm_slot(m):
                bl = fm_banks[0]
                i = bl[fm_i[0] % len(bl)]
                fm_i[0] += 1
                return PS[i][0:m, 0:128]

            def chunk_gen(c, bs):
                xc, qT, kT, zs, gs, vtok, ktok, dtp, dtA, loga = bs
                is_s = (c == NPC)
                last_p = (c == NPC - 1)
                tok0 = c * 128
                tok = slice(tok0, tok0 + 128)
                tri = TRIB if is_s else TRI
                uu = UB if is_s else U
                oo = ONESB if is_s else ones_f
                padS = pad.rearrange("p k (s t) -> p k s t", s=16)

                if c in rstd_ready:
                    rs_c = rstd_ready.pop(c)
                else:
                    rs_c = rstd_bufs[c % 2]
                    rms_rstd(tok0, 128, sq, rs_c, PS[0][:, 0:128])
                for kc in range(KC):
                    stt(hT[:, kc, :], xT[:, kc, tok], pv[:, PV_GM + kc:PV_GM + kc + 1], rs_c, ALU.mult, ALU.mult)
                if (not is_s) and c + 1 < NPC:
                    rs_n = rstd_bufs[(c + 1) % 2]
                    rms_rstd(tok0 + 128, 128, sq, rs_n, PS[0][:, 0:128])
                    rstd_ready[c + 1] = rs_n

                if is_s:
                    stc = t12
                    stc_sem = P.dsem()
                    dma("sp", stc[0:48, :], sconv_d.rearrange("s t c -> (s t) c"), stc_sem)
                    for cc in range(KC):
                        pst = fm_slot(128)
                        tr(pst[:, 0:48], stc[0:48, cc * 128:(cc + 1) * 128], ident[0:48, 0:48])
                        cp(padS[:, cc, :, 0:3], pst[:, 0:48].rearrange("p (s t) -> p s t", s=16), eng="act")

                def fm_proj(col0, m):
                    ps = fm_slot(m)
                    for kc in range(KC):
                        mm(ps, win[:, kc, col0:col0 + m], hT[:, kc, :], start=(kc == 0), stop=(kc == KC - 1))
                    return ps

                def conv_cc(cc):
                    ab_ = acc[cc % 2]
                    for i in range(3):
                        if is_s:
                            src_i = padS[:, cc, :, i:i + 8]
                            dst = ab_.rearrange("p (s t) -> p s t", s=16)
                        else:
                            src_i = pad[:, cc, i:i + 128]
                            dst = ab_
                        wcol = pv[:, PV_CW + i * 8 + cc: PV_CW + i * 8 + cc + 1]
                        stt(dst, src_i, wcol, dst, ALU.mult, ALU.add)
                    act(xc[:, cc, :], ab_, AF.Silu)

                def tm_proj(out, col0, n):
                    for kc in range(KC):
                        mm(out, hT[:, kc, :], win[:, kc, col0:col0 + n], start=(kc == 0), stop=(kc == KC - 1))

                fm_banks[0] = [0, 1]
                ps = fm_proj(3080, 16)
                cp(gkT[0:16, :], ps, eng="act")
                tm_proj(PS[5][:, 0:256], 1800, 256)
                tm_proj(PS[5][:, 256:264], 1536, 8)
                mm(PS[6][:, 0:256], gkT[0:16, :], wgk2[0:16, :])
                cp(ktok, PS[5][:, 0:256], eng="dve")
                tt(dtr, PS[5][:, 256:264], dtb_b, ALU.add)
                tt(xg, PS[6][:, 0:256], bgk_b, ALU.add)
                act(dtp, dtr, AF.Exp)
                act(dtp, dtp, AF.Ln, bias=1.0)
                act(loga, xg, AF.Exp, scale=-1.0)
                act(loga, loga, AF.Ln, bias=1.0)
                ts(loga, loga, -1.0 / 16.0, None, ALU.mult)
                tt(dtA, dtp, a_b, ALU.mult)
                fm_banks[0] = [0, 1, 5, 6]
                yield '.'

                for cc in range(KC):
                    ps = fm_proj(512 + cc * 128, 128)
                    if is_s:
                        cp(padS[:, cc, :, 3:11], ps.rearrange("p (s t) -> p s t", s=16), eng="act")
                    else:
                        cp(pad[:, cc, 3:131], ps, eng="act")
                    act(acc[cc % 2], ps, AF.Identity, bias=pv[:, PV_CB + cc:PV_CB + cc + 1],
                        scale=pv[:, PV_CW + 3 * 8 + cc: PV_CW + 3 * 8 + cc + 1])
                    if cc >= 1:
                        conv_cc(cc - 1)
                    yield '.'
                for h in range(4):
                    ps = fm_proj(1544 + h * 64, 64)
                    cp(qT[0:64, h, :], ps, eng="act")
                    if h == 0:
                        conv_cc(KC - 1)
                    yield '.'
                for h in range(4):
                    ps = fm_proj(1800 + h * 64, 64)
                    cp(kT[0:64, h, :], ps, eng="act")
                    yield '.'
                if not is_s:
                    cp(pad[:, :, 0:3], pad[:, :, 128:131])
                fm_banks[0] = [0, 1]

                yield 'P1'
                tm_proj(PS[0][:, 0:512], 0, 512)
                act(zs, PS[0][:, 0:512], AF.Silu)
                yield '.'
                tm_proj(PS[1][:, 0:512], 2056, 512)
                cp(vtok, PS[1][:, 0:512], eng="act")
                yield '.'
                tm_proj(PS[5][:, 0:512], 2568, 512)
                act(gs, PS[5][:, 0:512], AF.Silu)
                act(ssq2, epsc.to_broadcast([128, 2]), AF.Ln)

                yield 'P2'
                if PROBE_SKIP_S and not is_s:
                    yield 'S1'
                    yield 'S2'
                    yield 'S3a'
                    yield 'S3'
                    return
                if (not is_s) and c >= 1 and N_WARM > 0:
                    for i in range(N_WARM):
                        mm(PS[1][:, 0:512], identb, win[:, i % KC, 0:512])
                tt(R.rearrange("p (h i) -> p h i", h=8), bc(tri.unsqueeze(1), [128, 8, 128]),
                   bc(dtA.unsqueeze(2), [128, 8, 128]), ALU.mult)
                yield '.'
                mm(PS[1][:, 0:512], uu, R[:, 0:512])
                mm(PS[7][:, 0:512], uu, R[:, 512:1024])
                mm(PS[5][:, 264:272], tri, dtA)
                mm(PS[5][:, 272:280], uu, dtA)
                mm(PS[5][:, 280:288], oo, dtA)
                for h in range(4):
                    mm(PS[6][0:64, h * 128:(h + 1) * 128], loga[:, h * 64:(h + 1) * 64], tri)
                mm(PS[0][:, 0:256], uu, loga)
                yield '.'
                act(decT[:, 0:512], PS[1][:, 0:512], AF.Exp)
                act(decT[:, 512:1024], PS[7][:, 0:512], AF.Exp)
                act(E3, PS[5][:, 264:288], AF.Exp)
                Ecol = E3[:, 0:8]
                Wcol = E3[:, 8:16]
                DEC = E3[:, 16:24]

                eqf = eq[0:64].rearrange("p a b -> p (a b)")
                ekf = ek[0:64].rearrange("p a b -> p (a b)")
                act(eqf, PS[6][0:64, 0:512], AF.Exp)
                act(ekf, PS[6][0:64, 0:512], AF.Exp, scale=-1.0)
                act(esuf, PS[0][:, 0:256], AF.Exp)
                yield '.'
                stt(qtil[0:64].rearrange("p a b -> p (a b)"), qT[0:64].rearrange("p a b -> p (a b)"), 0.125, eqf, ALU.mult, ALU.mult)
                tt(ktil[0:64].rearrange("p a b -> p (a b)"), kT[0:64].rearrange("p a b -> p (a b)"), ekf, ALU.mult)
                tt(kd, ktok, esuf, ALU.mult)
                yield '.'
                for h in range(4):
                    mm(PS[5][:, h * 128:(h + 1) * 128], ktil[0:64, h, :], qtil[0:64, h, :])
                for g in range(2):
                    mm(PS[6][:, 256 + g * 128:256 + (g + 1) * 128], xc[:, 4 + g, :], xc[:, 6 + g, :])
                yield '.'
                tt(attT, PS[5][:, 0:512].rearrange("p (h i) -> p h i", h=4), bc(tri.unsqueeze(1), [128, 4, 128]), ALU.mult)
                tt(GTm, PS[6][:, 256:512].rearrange("p (g i) -> p g i", g=2), bc(tri.unsqueeze(1), [128, 2, 128]), ALU.mult)
                for g in range(2):
                    tt(scT[:, 4 * g:4 * g + 4, :], decT[:, 512 * g:512 * (g + 1)].rearrange("p (h i) -> p h i", h=4),
                       bc(GTm[:, g, :].unsqueeze(1), [128, 4, 128]), ALU.mult)

                yield '.'
                for cc in range(4):
                    tr(PS1b[:, cc * 128:(cc + 1) * 128], xc[:, cc, :], identb)
                for g in range(2):
                    tr(PS1b[:, 512 + g * 128:512 + (g + 1) * 128], xc[:, 4 + g, :], identb)
                yield '.'
                cp(xstok, PS1b[:, 0:512], eng="act")
                cp(Btok, PS1b[:, 512:768], eng="act")
                yield '.'
                tt(xdt.rearrange("p (h q) -> p h q", h=8), xstok.rearrange("p (h q) -> p h q", h=8),
                   bc(dtp.unsqueeze(2), [128, 8, 64]), ALU.mult)
                tt(xdtw.rearrange("p (h q) -> p h q", h=8), xdt.rearrange("p (h q) -> p h q", h=8),
                   bc(Wcol.unsqueeze(2), [128, 8, 64]), ALU.mult)

                yield 'S1'
                for h in range(8):
                    mm(PS[2][:, h * 64:(h + 1) * 64], scT[:, h, :], xdt[:, h * 64:(h + 1) * 64])
                if not is_s:
                    for g in range(2):
                        mm(PS[4][:, g * 256:(g + 1) * 256], xc[:, 6 + g, :], STb[:, g * 256:(g + 1) * 256])
                else:
                    mk_s = M.mark()
                    st1 = [M.f32(4, 128) for _ in range(2)]
                    st1_sem = [P.dsem() for _ in range(2)]
                    STs = STb
                    ystT = ST
                    Bm = [M.bf16(256) for _ in range(2)]
                    rh = M.f32(2, 64)
                    decS = M.f32(16, 4)
                    stn = [M.f32(4, 128)] * 2
                    stn_sem = [P.dsem() for _ in range(2)]
                    store_sems.extend(stn_sem)
                    for h2 in range(2):
                        tt(rh[:, h2, :].rearrange("p (s q) -> p s q", s=16),
                           bc(dtA.rearrange("p (q t) -> p q t", t=2)[:, :, h2].unsqueeze(1), [128, 16, 4]),
                           bc(SEQ.unsqueeze(2), [128, 16, 4]), ALU.mult)
                        mm(PS[5][h2 * 64:(h2 + 1) * 64, 320:384], ones_f[:, 0:64], rh[:, h2, :])
                    act(decS.rearrange("p s q -> p (s q)"), PS[5][:, 320:384], AF.Exp)
                    for s_ in range(16):
                        b = s_ % 2
                        dma("pool", st1[b], sssd_d[s_].rearrange("(q t) p n -> (t p) q n", t=2), st1_sem[b])
                        for hp in range(4):
                            tr(PS[1][:, hp * 128:(hp + 1) * 128], st1[b][:, hp, :], ident)
                        cp(STs, PS[1][:, 0:512], eng="act")
                        for hp in range(4):
                            mm(PS[0][:, hp * 128 + 8 * s_: hp * 128 + 8 * s_ + 8], STs[:, hp * 128:(hp + 1) * 128],
                               xc[:, 6 + hp // 2, 8 * s_:8 * s_ + 8])
                        ts(Bm[b], Btok, SEQ[:, s_:s_ + 1], None, ALU.mult)
                        for hp in range(4):
                            g = hp // 2
                            mm(PS[4][:, hp * 128:(hp + 1) * 128], xdtw[:, hp * 128:(hp + 1) * 128], Bm[b][:, g * 128:(g + 1) * 128])
                        tt(stn[b], st1[b], bc(decS[:, s_, :].unsqueeze(2), [128, 4, 128]), ALU.mult)
                        tt(stn[b], stn[b], PS[4][:, 0:512].rearrange("p (q n) -> p q n", q=4), ALU.add)
                        dma("sp", ossd_s_d[s_].rearrange("(q t) p n -> (t p) q n", t=2), stn[b], stn_sem[b])
                    cp(ystT, PS[0][:, 0:512], eng="act")
                    for hp in range(4):
                        tr(PS[4][:, hp * 128:(hp + 1) * 128], ystT[:, hp * 128:(hp + 1) * 128], ident)
                yield '.'
                tt(t1.rearrange("p (h q) -> p h q", h=8), PS[4][:, 0:512].rearrange("p (h q) -> p h q", h=8),
                   bc(Ecol.unsqueeze(2), [128, 8, 64]), ALU.mult)
                tt(t1, t1, PS[2][:, 0:512], ALU.add)
                tt(t2.rearrange("p (h q) -> p h q", h=8), xstok.rearrange("p (h q) -> p h q", h=8),
                   bc(dsk_b.unsqueeze(2), [128, 8, 64]), ALU.mult)
                tt(t1, t1, t2, ALU.add)
                tt(t1, t1, zs, ALU.mult)
                yield '.'
                for g in range(2):
                    act(t2[:, g * 256:(g + 1) * 256], t1[:, g * 256:(g + 1) * 256], AF.Square, accum_out=ssq2[:, g:g + 1])
                act(rs2, ssq2, AF.Ln, bias=epsc, scale=1.0 / 256.0)
                act(rs2, rs2, AF.Exp, scale=-0.5)
                yield '.'
                for g in range(2):
                    stt(mtok[:, g * 256:(g + 1) * 256], t1[:, g * 256:(g + 1) * 256], rs2[:, g:g + 1],
                        gssd_b[:, g * 256:(g + 1) * 256], ALU.mult, ALU.mult)

                yield '.'
                if not is_s:
                    for g in range(2):
                        mm(PS[4][:, g * 256:(g + 1) * 256], Btok[:, g * 128:(g + 1) * 128], xdtw[:, g * 256:(g + 1) * 256])
                    yield '.'
                    tt(ST.rearrange("p (h q) -> p h q", h=8), ST.rearrange("p (h q) -> p h q", h=8),
                       bc(DEC.unsqueeze(2), [128, 8, 64]), ALU.mult)
                    tt(ST, ST, PS[4][:, 0:512], ALU.add)
                    yield '.'
                    cp(STb, ST, eng="act")
                    if last_p:
                        for hp in range(4):
                            tr(PS[4][:, hp * 128:(hp + 1) * 128], ST[:, hp * 128:(hp + 1) * 128], ident)
                        cp(ostage.rearrange("p q n -> p (q n)"), PS[4][:, 0:512], eng="act")
                        osem = P.dsem()
                        store_sems.append(osem)
                        dma("sp", ossd_p_d.rearrange("(q t) p n -> (t p) q n", t=2), ostage, osem)

                yield 'S2'
                if not is_s:
                    for h in range(4):
                        mm(PS[3][:, h * 128:(h + 1) * 128], attT[:, h, :], vtok[:, h * 128:(h + 1) * 128], start=True, stop=False)
                        mm(PS[3][:, h * 128:(h + 1) * 128], qtil[0:64, h, :], SGb[0:64, h, :], start=False, stop=True)
                    yield '.'
                else:
                    M.reset(mk_s)
                    sg1 = [M.f32(4, 128) for _ in range(2)]
                    sg1_sem = [P.dsem() for _ in range(2)]
                    sg1b = SGb
                    ostT = SG.rearrange("p a b -> p (a b)")
                    kdm = [M.bf16(256) for _ in range(2)]
                    decG = M.f32(4, 16)
                    sgn = [M.f32(4, 128) for _ in range(2)]
                    sgn_sem = [P.dsem() for _ in range(2)]
                    store_sems.extend(sgn_sem)
                    for h in range(4):
                        mm(PS[1][0:64, h * 16:(h + 1) * 16], loga[:, h * 64:(h + 1) * 64], SEQ)
                    act(decG[0:64].rearrange("p a b -> p (a b)"), PS[1][0:64, 0:64], AF.Exp)
                    for s_ in range(16):
                        b = s_ % 2
                        dma("pool", sg1[b][0:64], sgla_d[s_].rearrange("h d v -> d h v"), sg1_sem[b])
                        cp(sg1b[0:64].rearrange("p a b -> p (a b)"), sg1[b][0:64].rearrange("p a b -> p (a b)"), eng="act")
                        for h in range(4):
                            mm(PS[0][:, h * 128 + 8 * s_: h * 128 + 8 * s_ + 8], sg1b[0:64, h, :], qtil[0:64, h, 8 * s_:8 * s_ + 8])
                        ts(kdm[b], kd, SEQ[:, s_:s_ + 1], None, ALU.mult)
                        for h in range(4):
                            mm(PS[7][0:64, h * 128:(h + 1) * 128], kdm[b][:, h * 64:(h + 1) * 64], vtok[:, h * 128:(h + 1) * 128])
                        for h in range(4):
                            stt(sgn[b][0:64, h, :], sg1[b][0:64, h, :], decG[0:64, h, s_:s_ + 1], PS[7][0:64, h * 128:(h + 1) * 128],
                                ALU.mult, ALU.add)
                        dma("sp", ogla_s_d[s_].rearrange("h d v -> d h v"), sgn[b][0:64], sgn_sem[b])
                    cp(ostT, PS[0][:, 0:512], eng="act")
                    for h in range(4):
                        mm(PS[3][:, h * 128:(h + 1) * 128], attT[:, h, :], vtok[:, h * 128:(h + 1) * 128], start=True, stop=False)
                        mm(PS[3][:, h * 128:(h + 1) * 128], ostT[:, h * 128:(h + 1) * 128], ident, start=False, stop=True)
                if not is_s:
                    for h in range(4):
                        mm(PS[7][0:64, h * 128:(h + 1) * 128], kd[:, h * 64:(h + 1) * 64], vtok[:, h * 128:(h + 1) * 128])
                    yield '.'
                    for h in range(4):
                        stt(SG[0:64, h, :], SG[0:64, h, :], eq[0:64, h, 127:128], PS[7][0:64, h * 128:(h + 1) * 128], ALU.mult, ALU.add)
                    cp(SGb[0:64].rearrange("p a b -> p (a b)"), SG[0:64].rearrange("p a b -> p (a b)"), eng="act")
                    if last_p:
                        gsem = P.dsem()
                        store_sems.append(gsem)
                        dma("sp", ogla_p_d.rearrange("h d v -> d h v"), SG[0:64], gsem)
                yield 'S3a'
                for h in range(4):
                    act(acc[0], PS[3][:, h * 128:(h + 1) * 128], AF.Square, accum_out=ssq4[:, h:h + 1])
                act(rs4, ssq4, AF.Ln, bias=epsc, scale=1.0 / 128.0)
                act(rs4, rs4, AF.Exp, scale=-0.5)
                yield '.'
                tt(gs.rearrange("p (h v) -> p h v", h=4), gs.rearrange("p (h v) -> p h v", h=4),
                   bc(ggla_b.unsqueeze(1), [128, 4, 128]), ALU.mult)
                for h in range(4):
                    stt(mtok[:, 512 + h * 128:512 + (h + 1) * 128], PS[3][:, h * 128:(h + 1) * 128], rs4[:, h:h + 1],
                        gs[:, h * 128:(h + 1) * 128], ALU.mult, ALU.mult)

                yield '.'
                if last_p or is_s:
                    tm_proj(PS[1][:, 0:512], 512, 512)
                    tm_proj(PS[2][:, 0:512], 1024, 512)
                    cp(cvo[:, 0:512], PS[1][:, 0:512], eng="act")
                    cp(cvo[:, 512:1024], PS[2][:, 0:512], eng="act")
                    csem = P.dsem()
                    store_sems.append(csem)
                    if last_p:
                        dma("sp", oconv_p_d, cvo[125:128, :], csem)
                    else:
                        for s_ in range(16):
                            dma("sp", oconv_s_d[s_], cvo[8 * s_ + 5:8 * s_ + 8, :], csem)

                par = 0 if is_s else (c % 2)
                mT = mT2[:, par]
                for kc in range(KC):
                    tr(PS2b[:, kc * 128:(kc + 1) * 128], mtok[:, kc * 128:(kc + 1) * 128], identb)
                cp(mT.rearrange("p a b -> p (a b)"), PS2b[:, 0:1024], eng="act")
                yield '.'
                if is_s:
                    for dc in range(KC):
                        po = PS[3 if dc % 2 == 0 else 4][:, 0:128]
                        for kc in range(KC):
                            mm(po, wout[:, kc, dc * 128:(dc + 1) * 128], mT[:, kc, :], start=(kc == 0), stop=(kc == KC - 1))
                        tt(xT[:, dc, tok], xT[:, dc, tok], po, ALU.add)
                        yield '.'
                elif par == 1:
                    tok2 = slice(tok0 - 128, tok0 + 128)
                    for dc in range(KC):
                        pb = PS[3 if dc % 2 == 0 else 4]
                        po3 = pb[:, 0:256].rearrange("p (a b) -> p a b", a=2)
                        for kc in range(KC):
                            mm(po3, wout[:, kc, dc * 128:(dc + 1) * 128], mT2[:, :, kc, :], start=(kc == 0), stop=(kc == KC - 1))
                        tt(xT[:, dc, tok2], xT[:, dc, tok2], pb[:, 0:256], ALU.add)
                        yield '.'
                yield 'S3'

            def run_all(g):
                for _ in g:
                    pass

            mk_sets = M.mark()
            if not DBG_SKIP_SAMPLE:
                run_all(chunk_gen(NPC, setA))
            M.reset(mk_sets)
            setB = alloc_set()
            memset(ST, 0.0)
            memset(STb, 0.0)
            memset(SG, 0.0)
            memset(SGb, 0.0)
            memset(pad, 0.0)
            sets = [setA, setB]
            gens = [chunk_gen(c, sets[c % 2]) for c in range(NPC)]

            def run_to(g, end):
                while True:
                    r = next(g)
                    if r == end:
                        return
                    assert r == '.', (r, end)

            def interleave(A, endA, B, endB, passB=(), ratio=1, passA=()):
                a_done = b_done = False
                while not (a_done and b_done):
                    if not b_done:
                        r = next(B)
                        if r == endB:
                            b_done = True
                        else:
                            assert r == '.' or r in passB, (r, endB)
                    for _ in range(ratio):
                        if not a_done:
                            r = next(A)
                            if r == endA:
                                a_done = True
                            else:
                                assert r == '.' or r in passA, (r, endA)

            run_to(gens[0], 'P1')
            run_to(gens[0], 'P2')
            run_to(gens[0], 'S1')
            for c in range(NPC):
                if c + 1 < NPC:
                    if FINE_INTERLEAVE:
                        interleave(gens[c + 1], 'P2', gens[c], 'S3a', passB=('S2',), ratio=IL_RATIO, passA=('P1',))
                        interleave(gens[c + 1], 'S1', gens[c], 'S3')
                    else:
                        run_to(gens[c + 1], 'P1')
                        run_to(gens[c + 1], 'P2')
                        run_to(gens[c], 'S2')
                        run_to(gens[c], 'S3a')
                        run_to(gens[c + 1], 'S1')
                        run_to(gens[c], 'S3')
                else:
                    run_to(gens[c], 'S2')
                    run_to(gens[c], 'S3a')
                    run_to(gens[c], 'S3')
            M.reset(mk)

        if do_mixer:
            mixer()

        TGF = 512
        sq_f = M.bf16(2, TGF)
        rstd_f = M.f32(TGF)
        yb = [M.f32(TGF) for _ in range(4)]
        ysem = [P.dsem() for _ in range(4)]
        store_sems += ysem
        yi = [0]

        def final_tg(t0, n):
            rms_rstd(t0, n, sq_f, rstd_f, PS[7][:, 0:n])
            for kc in range(KC):
                b = yi[0] % 4
                yi[0] += 1
                stt(yb[b][:, 0:n], xT[:, kc, t0:t0 + n], pv[:, PV_GF + kc: PV_GF + kc + 1], rstd_f[:, 0:n], ALU.mult, ALU.mult)
                dma("sp", yT_d[kc * 128:(kc + 1) * 128, t0:t0 + n], yb[b][:, 0:n], ysem[b])

        if do_ffn2:
            ffn(w2i_d, w2o_d, PV_G2, "f2", on_done=final_tg)
        else:
            for t0 in range(0, NT, TGF):
                final_tg(t0, min(TGF, NT - t0))

        P.wait_all("sp", store_sems)

        P.finalize_counts()
        for d in P.dsems:
            d.h = es.enter_context(nc.semaphore("d%d" % d.idx))
        esem = {}
        for e in Prog.ENGS:
            esem[e] = [es.enter_context(nc.semaphore("e_%s_%d" % (e, i))) for i in range(P.nsem[e])]
        with nc.Block() as block:
            @block.tensor
            def _(e):
                P.emit_engine("pe", e, esem)

            @block.scalar
            def _(e):
                P.emit_engine("act", e, esem)

            @block.vector
            def _(e):
                P.emit_engine("dve", e, esem)

            @block.gpsimd
            def _(e):
                P.emit_engine("pool", e, esem)

            @block.sync
            def _(e):
                P.emit_engine("sp", e, esem)
    stats = {k: len(v) for k, v in P.ops.items()}
    stats["sbuf_top"] = M.top
    stats["n_dsem"] = len(P.dsems)
    stats["n_esem"] = sum(P.nsem.values())
    return nc, stats


def prep_core_inputs(inp, core, PSEQ=2048):
    xp = np.asarray(inp["x_prompt"][core, :PSEQ], np.float32)
    xs = np.asarray(inp["x_sample"][16 * core:16 * core + 16], np.float32).reshape(128, D)
    x = np.concatenate([xp, xs], axis=0)
    return {
        "xT": np.ascontiguousarray(x.T),
        "st_ssd": np.ascontiguousarray(inp["state_ssd"][0, 16 * core:16 * core + 16]),
        "st_conv": np.ascontiguousarray(inp["state_conv"][0, 16 * core:16 * core + 16]),
        "st_gla": np.ascontiguousarray(inp["state_gla"][0, 16 * core:16 * core + 16]),
    }


def shared_inputs(inp):
    return {
        "w1i": np.ascontiguousarray(inp["w_ffn1_in"][0], np.float32),
        "w1o": np.ascontiguousarray(inp["w_ffn1_out"][0], np.float32),
        "wi": np.ascontiguousarray(inp["w_in"][0], np.float32),
        "wo": np.ascontiguousarray(inp["w_out"][0], np.float32),
        "w2i": np.ascontiguousarray(inp["w_ffn2_in"][0], np.float32),
        "w2o": np.ascontiguousarray(inp["w_ffn2_out"][0], np.float32),
        "consts": make_consts(),
        "pvec": make_pvec(inp),
        "wgk2": np.ascontiguousarray(inp["w_gk2"][0], np.float32),
    }


def run(inp, PSEQ=2048, ncores=NCORES, trace=False, **flags):
    nc, stats = build(PSEQ=PSEQ, **flags)
    sh = shared_inputs(inp)
    in_maps = []
    for c in range(ncores):
        m = dict(sh)
        m.update(prep_core_inputs(inp, c, PSEQ))
        in_maps.append(m)
    res = run_bass_kernel_spmd(nc, in_maps, core_ids=list(range(ncores)), trace=trace)
    return res, stats


def assemble(res, PSEQ=2048, ncores=NCORES):
    R = res.results
    yp = np.stack([np.asarray(R[c]["yT"])[:, :PSEQ].T for c in range(ncores)])
    ys = np.concatenate([np.asarray(R[c]["yT"])[:, PSEQ:].T.reshape(16, 8, D) for c in range(ncores)])
    ssd_p = np.stack([np.asarray(R[c]["o_ssd_p"]) for c in range(ncores)])[None]
    conv_p = np.stack([np.asarray(R[c]["o_conv_p"]) for c in range(ncores)])[None]
    gla_p = np.stack([np.asarray(R[c]["o_gla_p"]) for c in range(ncores)])[None]
    ssd_s = np.concatenate([np.asarray(R[c]["o_ssd_s"]) for c in range(ncores)])[None]
    conv_s = np.concatenate([np.asarray(R[c]["o_conv_s"]) for c in range(ncores)])[None]
    gla_s = np.concatenate([np.asarray(R[c]["o_gla_s"]) for c in range(ncores)])[None]
    return tuple(np.ascontiguousarray(a, dtype=np.float32) for a in
                 (yp, ys, ssd_p, conv_p, gla_p, ssd_s, conv_s, gla_s))


def kernel(**inputs):
    res, _ = run(inputs)
    return assemble(res)
```

```python
import numpy as np
import ml_dtypes
import concourse.bass as bass
import concourse.mybir as mybir
from concourse.bass_utils import run_bass_kernel_spmd

F32 = mybir.dt.float32
BF16 = mybir.dt.bfloat16
AF = mybir.ActivationFunctionType
ALU = mybir.AluOpType

NCORES = 8
D = 1024
KC = 8
DFF = 2816
MH = 22
INC = 3096
EPS = 1e-6
SBUF_F32 = 53100
EPOCH = 8000
DBG_STOP = 99
DBG_B = 3
N_WARM = 0
PROBE_SKIP_S = False
FINE_INTERLEAVE = True
IL_RATIO = 2
DBG_SKIP_SAMPLE = False


def _esize(dt):
    return mybir.dt.size(dt)


class DSem:
    def __init__(self, idx):
        self.idx = idx
        self.count = 0
        self.h = None


class Op:
    __slots__ = ("eng", "fn", "waits", "signal", "seq", "dsem", "cnt")


class Prog:
    ENGS = ("pe", "act", "dve", "pool", "sp")

    def __init__(self):
        self.ops = {e: [] for e in self.ENGS}
        self.known = {e: {} for e in self.ENGS}
        self.track = {}
        self.dsems = []

    def dsem(self):
        d = DSem(len(self.dsems))
        self.dsems.append(d)
        return d

    @staticmethod
    def _region(ap):
        pat = ap.ap
        es = _esize(ap.dtype)
        pstep = pat[0][0]
        off = ap.offset
        if pstep == 0:
            p0, f0 = 0, off
        else:
            p0, f0 = off // pstep, off % pstep
        ext = 1
        for (s, c) in pat[1:]:
            ext += (c - 1) * abs(s)
        return (p0, p0 + pat[0][1], f0 * es, (f0 + ext) * es)

    def _deps(self, ap, is_write, me, deps):
        name = ap.name
        if name.startswith("ps"):
            reg = (0, 128, 0, 2048)
            is_write = True
        else:
            reg = self._region(ap)
        lst = self.track.get(name, [])
        keep = []
        exact = None
        for ent in lst:
            if ent[0] < reg[1] and reg[0] < ent[1] and ent[2] < reg[3] and reg[2] < ent[3]:
                if ent[4] is not None:
                    deps.append((ent[4], "WAW" if is_write else "RAW"))
                if is_write:
                    for ev in ent[5].values():
                        deps.append((ev, "WAR"))
                    if reg[0] <= ent[0] and ent[1] <= reg[1] and reg[2] <= ent[2] and ent[3] <= reg[3]:
                        continue
                elif (ent[0], ent[1], ent[2], ent[3]) == reg:
                    exact = ent
            keep.append(ent)
        if is_write:
            keep.append([reg[0], reg[1], reg[2], reg[3], me, {}])
        else:
            if exact is None:
                exact = [reg[0], reg[1], reg[2], reg[3], None, {}]
                keep.append(exact)
            exact[5][me[0]] = me
        self.track[name] = keep

    def add(self, eng, fn, reads=(), writes=(), dsem=None):
        lst = self.ops[eng]
        op = Op()
        op.eng = eng
        op.fn = fn
        op.signal = False
        op.dsem = dsem
        op.cnt = 0
        op.seq = len(lst) + 1
        if dsem is not None:
            dsem.count += 16
            me = (dsem, dsem.count)
        else:
            me = (eng, op.seq)
        deps = []
        for ap in reads:
            if ap.name in TRACKED:
                self._deps(ap, False, me, deps)
        for ap in writes:
            if ap.name in TRACKED:
                self._deps(ap, True, me, deps)
        need = {}
        for (ev, kind) in deps:
            key, seq = ev
            if ev == me:
                continue
            if key == eng and eng == "pe":
                continue
            if need.get(key, 0) < seq:
                need[key] = seq
        kn = self.known[eng]
        waits = []
        for key, seq in need.items():
            if kn.get(key, 0) >= seq:
                continue
            kn[key] = seq
            waits.append((key, seq))
            if isinstance(key, DSem) and seq != key.count - (16 if key is dsem else 0):
                raise AssertionError("partial DMA-semaphore wait (%d of %d) on dsem %d" % (seq, key.count, key.idx))
            if not isinstance(key, DSem):
                self.ops[key][seq - 1].signal = True
        op.waits = waits
        lst.append(op)
        return op

    def wait_all(self, eng, dsems):
        op = Op()
        op.eng = eng
        op.fn = None
        op.signal = False
        op.dsem = None
        op.cnt = 0
        op.seq = len(self.ops[eng]) + 1
        op.waits = [(d, d.count) for d in dsems if d.count > 0]
        self.ops[eng].append(op)

    def finalize_counts(self):
        self.nsem = {}
        for e in self.ENGS:
            c = 0
            for op in self.ops[e]:
                if op.signal:
                    c += 1
                    op.cnt = c
            self.nsem[e] = max(1, (c + EPOCH - 1) // EPOCH)

    def emit_engine(self, e_name, e, esem):
        for op in self.ops[e_name]:
            for (key, seq) in op.waits:
                if isinstance(key, DSem):
                    e.wait_ge(key.h, seq)
                else:
                    p = self.ops[key][seq - 1]
                    c = p.cnt
                    e.wait_ge(esem[key][(c - 1) // EPOCH], (c - 1) % EPOCH + 1)
            if op.fn is None:
                continue
            ins = op.fn(e)
            if op.dsem is not None:
                ins.then_inc(op.dsem.h, 16)
            elif op.signal:
                c = op.cnt
                ins.then_inc(esem[e_name][(c - 1) // EPOCH], 1)


TRACKED = {"S"} | {"ps%d" % i for i in range(8)}


class Mem:
    def __init__(self, S):
        self.S = S
        self.top = 0
        self.limit = SBUF_F32 * 4

    def mark(self):
        return self.top

    def reset(self, m):
        self.top = m

    def _alloc(self, nbytes):
        off = self.top
        self.top += (nbytes + 31) // 32 * 32
        assert self.top <= self.limit, ("SBUF overflow", self.top, self.limit)
        return off

    def f32(self, *shape):
        n = int(np.prod(shape))
        off = self._alloc(n * 4)
        v = self.S[:, off // 4: off // 4 + n]
        return self._shape(v, shape)

    def bf16(self, *shape):
        n = int(np.prod(shape))
        n2 = (n + 1) // 2
        off = self._alloc(n2 * 4)
        v = self.S[:, off // 4: off // 4 + n2].bitcast(BF16)
        if n2 * 2 != n:
            v = v[:, 0:n]
        return self._shape(v, shape)

    @staticmethod
    def _shape(v, shape):
        if len(shape) == 1:
            return v
        if len(shape) == 2:
            return v.rearrange("p (a b) -> p a b", a=shape[0])
        if len(shape) == 3:
            return v.rearrange("p (a b c) -> p a b c", a=shape[0], b=shape[1])
        raise ValueError(shape)


def bc(ap, shape):
    return ap.to_broadcast(list(shape))


C_ID, C_ONES, C_TRI, C_U, C_TRIB, C_UB, C_ONESB = [i * 128 for i in range(7)]
C_SEQ = 7 * 128
C_EPS = C_SEQ + 16
NCF = C_EPS + 8
PV_G1, PV_GM, PV_G2, PV_GF = 0, 8, 16, 24
PV_CW = 32
PV_CB = 64
PV_DTB = 72
PV_ALOG = 80
PV_DSK = 88
PV_GSSD = 96
PV_GGLA = 608
PV_BGK = 736
NPV = 992


def make_consts():
    c = np.zeros((128, NCF), np.float32)
    k = np.arange(128)[:, None]
    i = np.arange(128)[None, :]
    same = (k // 8) == (i // 8)
    c[:, C_ID:C_ID + 128] = (k == i)
    c[:, C_ONES:C_ONES + 128] = 1.0
    c[:, C_TRI:C_TRI + 128] = (k <= i)
    c[:, C_U:C_U + 128] = (k > i)
    c[:, C_TRIB:C_TRIB + 128] = (k <= i) & same
    c[:, C_UB:C_UB + 128] = (k > i) & same
    c[:, C_ONESB:C_ONESB + 128] = same
    c[:, C_SEQ:C_SEQ + 16] = (k // 8) == np.arange(16)[None, :]
    c[:, C_EPS] = EPS
    return c


def make_pvec(inp):
    pv = np.zeros((128, NPV), np.float32)

    def fm(v):
        return np.ascontiguousarray(np.asarray(v, np.float32).reshape(8, 128).T)

    pv[:, PV_G1:PV_G1 + 8] = fm(inp["g_ffn1"][0])
    pv[:, PV_GM:PV_GM + 8] = fm(inp["g_mix"][0])
    pv[:, PV_G2:PV_G2 + 8] = fm(inp["g_ffn2"][0])
    pv[:, PV_GF:PV_GF + 8] = fm(inp["g_final"])
    cw = np.asarray(inp["conv_w"][0], np.float32)
    for t in range(4):
        pv[:, PV_CW + t * 8:PV_CW + t * 8 + 8] = fm(cw[t])
    pv[:, PV_CB:PV_CB + 8] = fm(inp["conv_b"][0])
    pv[:, PV_DTB:PV_DTB + 8] = np.asarray(inp["dt_bias"][0], np.float32)[None, :]
    pv[:, PV_ALOG:PV_ALOG + 8] = np.asarray(inp["a_log"][0], np.float32)[None, :]
    pv[:, PV_DSK:PV_DSK + 8] = np.asarray(inp["d_skip"][0], np.float32)[None, :]
    pv[:, PV_GSSD:PV_GSSD + 512] = np.asarray(inp["g_ssd_norm"][0], np.float32)[None, :]
    pv[:, PV_GGLA:PV_GGLA + 128] = np.asarray(inp["g_gla_norm"][0], np.float32)[None, :]
    pv[:, PV_BGK:PV_BGK + 256] = np.asarray(inp["b_gk"][0], np.float32)[None, :]
    return pv


def build(PSEQ=2048, do_ffn1=True, do_mixer=True, do_ffn2=True, dbg=None):
    NPC = PSEQ // 128
    NCH = NPC + 1
    NT = NCH * 128
    nc = bass.Bass("TRN2", target_bir_lowering=False)

    def din(name, shape, dt=F32):
        return nc.dram_tensor(name, list(shape), dt, kind="ExternalInput").ap()

    def dout(name, shape, dt=F32):
        return nc.dram_tensor(name, list(shape), dt, kind="ExternalOutput").ap()

    xT_d = din("xT", [D, NT])
    w1i_d = din("w1i", [D, 2 * DFF])
    w1o_d = din("w1o", [DFF, D])
    wi_d = din("wi", [D, INC])
    wo_d = din("wo", [D, D])
    w2i_d = din("w2i", [D, 2 * DFF])
    w2o_d = din("w2o", [DFF, D])
    consts_d = din("consts", [128, NCF])
    pvec_d = din("pvec", [128, NPV])
    wgk2_d = din("wgk2", [16, 256])
    sssd_d = din("st_ssd", [16, 8, 64, 128])
    sconv_d = din("st_conv", [16, 3, 1024])
    sgla_d = din("st_gla", [16, 4, 64, 128])

    yT_d = dout("yT", [D, NT])
    ossd_p_d = dout("o_ssd_p", [8, 64, 128])
    oconv_p_d = dout("o_conv_p", [3, 1024])
    ogla_p_d = dout("o_gla_p", [4, 64, 128])
    ossd_s_d = dout("o_ssd_s", [16, 8, 64, 128])
    oconv_s_d = dout("o_conv_s", [16, 3, 1024])
    ogla_s_d = dout("o_gla_s", [16, 4, 64, 128])
    dbg_d = {}
    if dbg:
        for k, shp in dbg.items():
            dbg_d[k] = dout("dbg_" + k, shp)

    P = Prog()
    from contextlib import ExitStack
    es = ExitStack()
    with es:
        S = es.enter_context(nc.sbuf_tensor("S", [128, SBUF_F32], F32))
        PS = [es.enter_context(nc.psum_tensor("ps%d" % i, [128, 512], F32)) for i in range(8)]
        M = Mem(S)

        def mm(out, lhsT, rhs, start=True, stop=True):
            P.add("pe", lambda e: e.matmul(out, lhsT=lhsT, rhs=rhs, start=start, stop=stop),
                  reads=[lhsT, rhs], writes=[out])

        def tr(out, in_, ident):
            P.add("pe", lambda e: e.transpose(out, in_, ident), reads=[in_, ident], writes=[out])

        def act(out, in_, func, bias=None, scale=None, accum_out=None, eng="act"):
            kw = {}
            rd = [in_]
            wr = [out]
            if bias is not None:
                kw["bias"] = bias
                if not isinstance(bias, float):
                    rd.append(bias)
            if scale is not None:
                kw["scale"] = scale
                if not isinstance(scale, float):
                    rd.append(scale)
            if accum_out is not None:
                kw["accum_out"] = accum_out
                wr.append(accum_out)
            P.add(eng, lambda e: e.activation(out=out, in_=in_, func=func, **kw), reads=rd, writes=wr)

        def tt(out, in0, in1, op, eng="dve"):
            P.add(eng, lambda e: e.tensor_tensor(out=out, in0=in0, in1=in1, op=op), reads=[in0, in1], writes=[out])

        def ts(out, in0, s1, s2, op0, op1=None, eng="dve"):
            rd = [in0] + [s for s in (s1, s2) if s is not None and not isinstance(s, float)]
            if op1 is None:
                P.add(eng, lambda e: e.tensor_scalar(out=out, in0=in0, scalar1=s1, scalar2=None, op0=op0),
                      reads=rd, writes=[out])
            else:
                P.add(eng, lambda e: e.tensor_scalar(out=out, in0=in0, scalar1=s1, scalar2=s2, op0=op0, op1=op1),
                      reads=rd, writes=[out])

        def stt(out, in0, scalar, in1, op0, op1, eng="dve"):
            rd = [in0, in1] + ([] if isinstance(scalar, float) else [scalar])
            P.add(eng, lambda e: e.scalar_tensor_tensor(out=out, in0=in0, scalar=scalar, in1=in1, op0=op0, op1=op1),
                  reads=rd, writes=[out])

        def cp(out, in_, eng="dve"):
            if eng == "act":
                P.add("act", lambda e: e.copy(out=out, in_=in_), reads=[in_], writes=[out])
            else:
                P.add(eng, lambda e: e.tensor_copy(out=out, in_=in_), reads=[in_], writes=[out])

        def memset(ap, val, eng="dve"):
            P.add(eng, lambda e: e.memset(ap, val), writes=[ap])

        def recip(out, in_):
            P.add("dve", lambda e: e.reciprocal(out=out, in_=in_), reads=[in_], writes=[out])

        def dma(q, out, in_, ds):
            P.add(q, lambda e: e.dma_start(out=out, in_=in_), reads=[in_], writes=[out], dsem=ds)

        xT = M.f32(KC, NT)
        ARENA_E = KC * INC + KC * D
        arena = M.bf16(ARENA_E)
        cst = M.f32(NCF)
        pv = M.f32(NPV)
        wgk2 = M.f32(256)
        identb = M.bf16(128)
        onesb = M.bf16(128)
        base_mark = M.mark()

        ident = cst[:, C_ID:C_ID + 128]
        ones_f = cst[:, C_ONES:C_ONES + 128]
        epsc = cst[:, C_EPS:C_EPS + 1]

        dma("sp", cst, consts_d, P.dsem())
        dma("sp", pv, pvec_d, P.dsem())
        dma("sp", wgk2[0:16, :], wgk2_d, P.dsem())
        for t0 in range(0, NT, 512):
            n = min(512, NT - t0)
            for kc in range(KC):
                dma("sp", xT[:, kc, t0:t0 + n], xT_d[kc * 128:(kc + 1) * 128, t0:t0 + n], P.dsem())
        cp(identb, ident)
        cp(onesb, ones_f)

        store_sems = []

        def rms_rstd(tok0, n, sq, rstd, psum_ap):
            for kc in range(KC):
                act(sq[:, kc % 2, 0:n], xT[:, kc, tok0:tok0 + n], AF.Square)
                mm(psum_ap, onesb, sq[:, kc % 2, 0:n], start=(kc == 0), stop=(kc == KC - 1))
            act(rstd[:, 0:n], psum_ap, AF.Ln, bias=epsc, scale=1.0 / D)
            act(rstd[:, 0:n], rstd[:, 0:n], AF.Exp, scale=-0.5)

        def ffn(w_in_d, w_out_d, pv_g, tag, on_done=None):
            mk = M.mark()
            TG = 512
            ntg = (NT + TG - 1) // TG
            base = -(-NT // (ntg * 64)) * 64
            sizes = [base] * ntg
            sizes[-1] = NT - base * (ntg - 1)
            if sizes[-1] > TG:
                sizes = [min(TG, NT - t0) for t0 in range(0, NT, TG)]
            tgs = []
            t0_ = 0
            for n_ in sizes:
                tgs.append((t0_, n_))
                t0_ += n_
            xn = M.bf16(KC, NT)
            sq = M.bf16(2, TG)
            rstd = M.f32(TG)
            GMAX = 5
            groups = [2, 5, 5, 5, 5]
            assert sum(groups) == MH
            gT = [M.bf16(GMAX, TG) for _ in range(2)]
            sil = [M.f32(TG) for _ in range(2)]
            SLOT_E = ARENA_E // 2
            assert KC * 2 * GMAX * 128 + GMAX * D <= SLOT_E
            slot_sem = [[P.dsem() for _ in range(3)] for _ in range(2)]

            def slot_views(s, G):
                base = s * SLOT_E
                wi = arena[:, base: base + KC * 2 * G * 128].rearrange("p (k c) -> p k c", k=KC)
                wo = arena[:, base + KC * 2 * GMAX * 128: base + KC * 2 * GMAX * 128 + G * D].rearrange(
                    "p (m c) -> p m c", m=G)
                return wi, wo

            def load_group(gi, m0, G):
                s = gi % 2
                wi, wo = slot_views(s, G)
                src = w_in_d.rearrange("(k p) c -> p k c", p=128)
                dma("pool", wi[:, :, 0:G * 128], src[:, :, m0 * 128:(m0 + G) * 128], slot_sem[s][0])
                dma("pool", wi[:, :, G * 128:2 * G * 128], src[:, :, DFF + m0 * 128: DFF + (m0 + G) * 128], slot_sem[s][1])
                srco = w_out_d[m0 * 128:(m0 + G) * 128, :].rearrange("(m p) c -> p m c", p=128)
                dma("pool", wo, srco, slot_sem[s][2])

            def norm_tg(ti):
                t0, n = tgs[ti]
                rms_rstd(t0, n, sq, rstd, PS[6][:, 0:n])
                for kc in range(KC):
                    stt(xn[:, kc, t0:t0 + n], xT[:, kc, t0:t0 + n], pv[:, pv_g + kc: pv_g + kc + 1],
                        rstd[:, 0:n], ALU.mult, ALU.mult)

            m0s = np.cumsum([0] + groups)
            load_group(0, int(m0s[0]), groups[0])
            load_group(1, int(m0s[1]), groups[1])
            norm_tg(0)

            state = {"ab": 0, "ob": 0, "gb": 0}

            def stageA(gi, G, t0, n):
                wi, wo = slot_views(gi % 2, G)
                gt = gT[state["gb"] % 2]
                for m in range(G):
                    ab = state["ab"] % 2
                    state["ab"] += 1
                    pg = PS[0 + 2 * ab][:, 0:n]
                    pu = PS[1 + 2 * ab][:, 0:n]
                    for kc in range(KC):
                        mm(pg, wi[:, kc, m * 128:(m + 1) * 128], xn[:, kc, t0:t0 + n], start=(kc == 0), stop=(kc == KC - 1))
                    for kc in range(KC):
                        mm(pu, wi[:, kc, (G + m) * 128:(G + m + 1) * 128], xn[:, kc, t0:t0 + n], start=(kc == 0), stop=(kc == KC - 1))
                    sl = sil[ab]
                    act(sl[:, 0:n], pg, AF.Silu)
                    tt(gt[:, m, 0:n], sl[:, 0:n], pu, ALU.mult)
                return gt

            def stageB(gi, G, t0, n, gt):
                wi, wo = slot_views(gi % 2, G)
                for dc in range(KC):
                    ob = state["ob"] % 4
                    state["ob"] += 1
                    po = PS[4 + ob][:, 0:n]
                    for m in range(G):
                        mm(po, wo[:, m, dc * 128:(dc + 1) * 128], gt[:, m, 0:n], start=(m == 0), stop=(m == G - 1))
                    stt(xT[:, dc, t0:t0 + n], po, 0.5, xT[:, dc, t0:t0 + n], ALU.mult, ALU.add)
                if on_done is not None and gi == len(groups) - 1:
                    on_done(t0, n)

            pend = None
            for gi, G in enumerate(groups):
                for ti, (t0, n) in enumerate(tgs):
                    if gi == 0 and ti + 1 < len(tgs):
                        norm_tg(ti + 1)
                    gt = stageA(gi, G, t0, n)
                    state["gb"] += 1
                    if pend is not None:
                        stageB(*pend)
                    pend = (gi, G, t0, n, gt)
                    if ti == 0 and gi >= 1 and gi + 1 < len(groups):
                        load_group(gi + 1, int(m0s[gi + 1]), groups[gi + 1])
            stageB(*pend)
            M.reset(mk)

        if do_ffn1:
            ffn(w1i_d, w1o_d, PV_G1, "f1")


        def dbg_dump(name, ap):
            if name in dbg_d:
                ds = P.dsem()
                store_sems.append(ds)
                dma("sp", dbg_d[name], ap, ds)

        def mixer():
            mk = M.mark()
            win = arena[:, 0:KC * INC].rearrange("p (k c) -> p k c", k=KC)
            wout = arena[:, KC * INC:KC * INC + KC * D].rearrange("p (k c) -> p k c", k=KC)
            src = wi_d.rearrange("(k p) c -> p k c", p=128)
            for kc in range(KC):
                dma("pool", win[:, kc, :], src[:, kc, :], P.dsem())
            dma("pool", wout, wo_d.rearrange("(k p) c -> p k c", p=128), P.dsem())

            TRI = cst[:, C_TRI:C_TRI + 128]
            U = cst[:, C_U:C_U + 128]
            TRIB = cst[:, C_TRIB:C_TRIB + 128]
            UB = cst[:, C_UB:C_UB + 128]
            ONESB = cst[:, C_ONESB:C_ONESB + 128]
            SEQ = cst[:, C_SEQ:C_SEQ + 16]
            dtb_b = pv[:, PV_DTB:PV_DTB + 8]
            dsk_b = pv[:, PV_DSK:PV_DSK + 8]
            gssd_b = pv[:, PV_GSSD:PV_GSSD + 512]
            ggla_b = pv[:, PV_GGLA:PV_GGLA + 128]
            bgk_b = pv[:, PV_BGK:PV_BGK + 256]

            sq = M.bf16(2, 128)
            rstd = M.f32(128)
            rstd_bufs = [rstd, M.f32(128)]
            rstd_ready = {}
            pad = M.f32(KC, 176)
            acc = [M.f32(128) for _ in range(2)]
            gkT = M.f32(128)
            mT2 = M.bf16(2, KC, 128)
            mixedT = mT2[:, 0]
            hT = mT2[:, 1]
            a_b = M.f32(8)
            dtr = M.f32(8)
            E3 = M.f32(24)
            ssq2 = M.f32(2)
            rs2 = M.f32(2)
            ssq4 = M.f32(4)
            rs4 = M.f32(4)
            t12 = M.f32(1024)
            R = t12
            GTm = M.f32(2, 128)
            sc_raw = M.f32(512)
            scT = sc_raw.bitcast(BF16).rearrange("p (a b) -> p a b", a=8)
            xstok = M.bf16(512)
            xdt = M.bf16(512)
            xdtw = M.bf16(512)
            Btok = M.bf16(256)
            t1 = t12[:, 0:512]
            t2 = t12[:, 512:1024]
            ST = M.f32(512)
            STb = M.bf16(512)
            eq = M.f32(4, 128)
            ek = M.f32(4, 128)
            qtil = M.bf16(4, 128)
            ktil = M.bf16(4, 128)
            esuf = M.f32(256)
            xg = esuf
            kd = M.bf16(256)
            attT = M.bf16(4, 128)
            mtok = M.bf16(1024)
            decT = t12[:, 0:512].bitcast(BF16)
            SG = M.f32(4, 128)
            SGb = M.bf16(4, 128)
            ostage = sc_raw.rearrange("p (q n) -> p q n", q=4)
            cvo = t12

            def alloc_set():
                return (M.bf16(KC, 128), M.f32(4, 128), M.f32(4, 128), M.f32(512), M.f32(512), M.bf16(512), M.f32(256),
                        M.f32(8), M.f32(8), M.f32(256))

            setA = alloc_set()
            PS1b = PS[1][:].bitcast(BF16)
            PS6b = PS[6][:].bitcast(BF16)
            PS2b = PS[2][:].bitcast(BF16)

            act(a_b, pv[:, PV_ALOG:PV_ALOG + 8], AF.Exp)
            ts(a_b, a_b, -1.0, None, ALU.mult)
            memset(pad, 0.0)

            fm_i = [0]

            fm_banks = [[0, 1]]

            def fm_slot(m):
                bl = fm_banks[0]
                i = bl[fm_i[0] % len(bl)]
                fm_i[0] += 1
                return PS[i][0:m, 0:128]

            def chunk_gen(c, bs):
                xc, qT, kT, zs, gs, vtok, ktok, dtp, dtA, loga = bs
                is_s = (c == NPC)
                last_p = (c == NPC - 1)
                tok0 = c * 128
                tok = slice(tok0, tok0 + 128)
                tri = TRIB if is_s else TRI
                uu = UB if is_s else U
                oo = ONESB if is_s else ones_f
                padS = pad.rearrange("p k (s t) -> p k s t", s=16)

                if c in rstd_ready:
                    rs_c = rstd_ready.pop(c)
                else:
                    rs_c = rstd_bufs[c % 2]
                    rms_rstd(tok0, 128, sq, rs_c, PS[0][:, 0:128])
                for kc in range(KC):
                    stt(hT[:, kc, :], xT[:, kc, tok], pv[:, PV_GM + kc:PV_GM + kc + 1], rs_c, ALU.mult, ALU.mult)
                if (not is_s) and c + 1 < NPC:
                    rs_n = rstd_bufs[(c + 1) % 2]
                    rms_rstd(tok0 + 128, 128, sq, rs_n, PS[0][:, 0:128])
                    rstd_ready[c + 1] = rs_n

                if is_s:
                    stc = t12
                    stc_sem = P.dsem()
                    dma("sp", stc[0:48, :], sconv_d.rearrange("s t c -> (s t) c"), stc_sem)
                    for cc in range(KC):
                        pst = fm_slot(128)
                        tr(pst[:, 0:48], stc[0:48, cc * 128:(cc + 1) * 128], ident[0:48, 0:48])
                        cp(padS[:, cc, :, 0:3], pst[:, 0:48].rearrange("p (s t) -> p s t", s=16), eng="act")

                def fm_proj(col0, m):
                    ps = fm_slot(m)
                    for kc in range(KC):
                        mm(ps, win[:, kc, col0:col0 + m], hT[:, kc, :], start=(kc == 0), stop=(kc == KC - 1))
                    return ps

                def conv_cc(cc):
                    ab_ = acc[cc % 2]
                    for i in range(3):
                        if is_s:
                            src_i = padS[:, cc, :, i:i + 8]
                            dst = ab_.rearrange("p (s t) -> p s t", s=16)
                        else:
                            src_i = pad[:, cc, i:i + 128]
                            dst = ab_
                        wcol = pv[:, PV_CW + i * 8 + cc: PV_CW + i * 8 + cc + 1]
                        stt(dst, src_i, wcol, dst, ALU.mult, ALU.add)
                    act(xc[:, cc, :], ab_, AF.Silu)

                def tm_proj(out, col0, n):
                    for kc in range(KC):
                        mm(out, hT[:, kc, :], win[:, kc, col0:col0 + n], start=(kc == 0), stop=(kc == KC - 1))

                fm_banks[0] = [0, 1]
                ps = fm_proj(3080, 16)
                cp(gkT[0:16, :], ps, eng="act")
                tm_proj(PS[5][:, 0:256], 1800, 256)
                tm_proj(PS[5][:, 256:264], 1536, 8)
                mm(PS[6][:, 0:256], gkT[0:16, :], wgk2[0:16, :])
                cp(ktok, PS[5][:, 0:256], eng="dve")
                tt(dtr, PS[5][:, 256:264], dtb_b, ALU.add)
                tt(xg, PS[6][:, 0:256], bgk_b, ALU.add)
                act(dtp, dtr, AF.Exp)
                act(dtp, dtp, AF.Ln, bias=1.0)
                act(loga, xg, AF.Exp, scale=-1.0)
                act(loga, loga, AF.Ln, bias=1.0)
                ts(loga, loga, -1.0 / 16.0, None, ALU.mult)
                tt(dtA, dtp, a_b, ALU.mult)
                fm_banks[0] = [0, 1, 5, 6]
                yield '.'

                for cc in range(KC):
                    ps = fm_proj(512 + cc * 128, 128)
                    if is_s:
                        cp(padS[:, cc, :, 3:11], ps.rearrange("p (s t) -> p s t", s=16), eng="act")
                    else:
                        cp(pad[:, cc, 3:131], ps, eng="act")
                    act(acc[cc % 2], ps, AF.Identity, bias=pv[:, PV_CB + cc:PV_CB + cc + 1],
                        scale=pv[:, PV_CW + 3 * 8 + cc: PV_CW + 3 * 8 + cc + 1])
                    if cc >= 1:
                        conv_cc(cc - 1)
                    yield '.'
                for h in range(4):
                    ps = fm_proj(1544 + h * 64, 64)
                    cp(qT[0:64, h, :], ps, eng="act")
                    if h == 0:
                        conv_cc(KC - 1)
                    yield '.'
                for h in range(4):
                    ps = fm_proj(1800 + h * 64, 64)
                    cp(kT[0:64, h, :], ps, eng="act")
                    yield '.'
                if not is_s:
                    cp(pad[:, :, 0:3], pad[:, :, 128:131])
                fm_banks[0] = [0, 1]

                yield 'P1'
                tm_proj(PS[0][:, 0:512], 0, 512)
                act(zs, PS[0][:, 0:512], AF.Silu)
                yield '.'
                tm_proj(PS[1][:, 0:512], 2056, 512)
                cp(vtok, PS[1][:, 0:512], eng="act")
                yield '.'
                tm_proj(PS[5][:, 0:512], 2568, 512)
                act(gs, PS[5][:, 0:512], AF.Silu)
                act(ssq2, epsc.to_broadcast([128, 2]), AF.Ln)

                yield 'P2'
                if PROBE_SKIP_S and not is_s:
                    yield 'S1'
                    yield 'S2'
                    yield 'S3a'
                    yield 'S3'
                    return
                if (not is_s) and c >= 1 and N_WARM > 0:
                    for i in range(N_WARM):
                        mm(PS[1][:, 0:512], identb, win[:, i % KC, 0:512])
                tt(R.rearrange("p (h i) -> p h i", h=8), bc(tri.unsqueeze(1), [128, 8, 128]),
                   bc(dtA.unsqueeze(2), [128, 8, 128]), ALU.mult)
                yield '.'
                mm(PS[1][:, 0:512], uu, R[:, 0:512])
                mm(PS[7][:, 0:512], uu, R[:, 512:1024])
                mm(PS[5][:, 264:272], tri, dtA)
                mm(PS[5][:, 272:280], uu, dtA)
                mm(PS[5][:, 280:288], oo, dtA)
                for h in range(4):
                    mm(PS[6][0:64, h * 128:(h + 1) * 128], loga[:, h * 64:(h + 1) * 64], tri)
                mm(PS[0][:, 0:256], uu, loga)
                yield '.'
                act(decT[:, 0:512], PS[1][:, 0:512], AF.Exp)
                act(decT[:, 512:1024], PS[7][:, 0:512], AF.Exp)
                act(E3, PS[5][:, 264:288], AF.Exp)
                Ecol = E3[:, 0:8]
                Wcol = E3[:, 8:16]
                DEC = E3[:, 16:24]

                eqf = eq[0:64].rearrange("p a b -> p (a b)")
                ekf = ek[0:64].rearrange("p a b -> p (a b)")
                act(eqf, PS[6][0:64, 0:512], AF.Exp)
                act(ekf, PS[6][0:64, 0:512], AF.Exp, scale=-1.0)
                act(esuf, PS[0][:, 0:256], AF.Exp)
                yield '.'
                stt(qtil[0:64].rearrange("p a b -> p (a b)"), qT[0:64].rearrange("p a b -> p (a b)"), 0.125, eqf, ALU.mult, ALU.mult)
                tt(ktil[0:64].rearrange("p a b -> p (a b)"), kT[0:64].rearrange("p a b -> p (a b)"), ekf, ALU.mult)
                tt(kd, ktok, esuf, ALU.mult)
                yield '.'
                for h in range(4):
                    mm(PS[5][:, h * 128:(h + 1) * 128], ktil[0:64, h, :], qtil[0:64, h, :])
                for g in range(2):
                    mm(PS[6][:, 256 + g * 128:256 + (g + 1) * 128], xc[:, 4 + g, :], xc[:, 6 + g, :])
                yield '.'
                tt(attT, PS[5][:, 0:512].rearrange("p (h i) -> p h i", h=4), bc(tri.unsqueeze(1), [128, 4, 128]), ALU.mult)
                tt(GTm, PS[6][:, 256:512].rearrange("p (g i) -> p g i", g=2), bc(tri.unsqueeze(1), [128, 2, 128]), ALU.mult)
                for g in range(2):
                    tt(scT[:, 4 * g:4 * g + 4, :], decT[:, 512 * g:512 * (g + 1)].rearrange("p (h i) -> p h i", h=4),
                       bc(GTm[:, g, :].unsqueeze(1), [128, 4, 128]), ALU.mult)

                yield '.'
                for cc in range(4):
                    tr(PS1b[:, cc * 128:(cc + 1) * 128], xc[:, cc, :], identb)
                for g in range(2):
                    tr(PS1b[:, 512 + g * 128:512 + (g + 1) * 128], xc[:, 4 + g, :], identb)
                yield '.'
                cp(xstok, PS1b[:, 0:512], eng="act")
                cp(Btok, PS1b[:, 512:768], eng="act")
                yield '.'
                tt(xdt.rearrange("p (h q) -> p h q", h=8), xstok.rearrange("p (h q) -> p h q", h=8),
                   bc(dtp.unsqueeze(2), [128, 8, 64]), ALU.mult)
                tt(xdtw.rearrange("p (h q) -> p h q", h=8), xdt.rearrange("p (h q) -> p h q", h=8),
                   bc(Wcol.unsqueeze(2), [128, 8, 64]), ALU.mult)

                yield 'S1'
                for h in range(8):
                    mm(PS[2][:, h * 64:(h + 1) * 64], scT[:, h, :], xdt[:, h * 64:(h + 1) * 64])
                if not is_s:
                    for g in range(2):
                        mm(PS[4][:, g * 256:(g + 1) * 256], xc[:, 6 + g, :], STb[:, g * 256:(g + 1) * 256])
                else:
                    mk_s = M.mark()
                    st1 = [M.f32(4, 128) for _ in range(2)]
                    st1_sem = [P.dsem() for _ in range(2)]
                    STs = STb
                    ystT = ST
                    Bm = [M.bf16(256) for _ in range(2)]
                    rh = M.f32(2, 64)
                    decS = M.f32(16, 4)
                    stn = [M.f32(4, 128)] * 2
                    stn_sem = [P.dsem() for _ in range(2)]
                    store_sems.extend(stn_sem)
                    for h2 in range(2):
                        tt(rh[:, h2, :].rearrange("p (s q) -> p s q", s=16),
                           bc(dtA.rearrange("p (q t) -> p q t", t=2)[:, :, h2].unsqueeze(1), [128, 16, 4]),
                           bc(SEQ.unsqueeze(2), [128, 16, 4]), ALU.mult)
                        mm(PS[5][h2 * 64:(h2 + 1) * 64, 320:384], ones_f[:, 0:64], rh[:, h2, :])
                    act(decS.rearrange("p s q -> p (s q)"), PS[5][:, 320:384], AF.Exp)
                    for s_ in range(16):
                        b = s_ % 2
                        dma("pool", st1[b], sssd_d[s_].rearrange("(q t) p n -> (t p) q n", t=2), st1_sem[b])
                        for hp in range(4):
                            tr(PS[1][:, hp * 128:(hp + 1) * 128], st1[b][:, hp, :], ident)
                        cp(STs, PS[1][:, 0:512], eng="act")
                        for hp in range(4):
                            mm(PS[0][:, hp * 128 + 8 * s_: hp * 128 + 8 * s_ + 8], STs[:, hp * 128:(hp + 1) * 128],
                               xc[:, 6 + hp // 2, 8 * s_:8 * s_ + 8])
                        ts(Bm[b], Btok, SEQ[:, s_:s_ + 1], None, ALU.mult)
                        for hp in range(4):
                            g = hp // 2
                            mm(PS[4][:, hp * 128:(hp + 1) * 128], xdtw[:, hp * 128:(hp + 1) * 128], Bm[b][:, g * 128:(g + 1) * 128])
                        tt(stn[b], st1[b], bc(decS[:, s_, :].unsqueeze(2), [128, 4, 128]), ALU.mult)
                        tt(stn[b], stn[b], PS[4][:, 0:512].rearrange("p (q n) -> p q n", q=4), ALU.add)
                        dma("sp", ossd_s_d[s_].rearrange("(q t) p n -> (t p) q n", t=2), stn[b], stn_sem[b])
                    cp(ystT, PS[0][:, 0:512], eng="act")
                    for hp in range(4):
                        tr(PS[4][:, hp * 128:(hp + 1) * 128], ystT[:, hp * 128:(hp + 1) * 128], ident)
                yield '.'
                tt(t1.rearrange("p (h q) -> p h q", h=8), PS[4][:, 0:512].rearrange("p (h q) -> p h q", h=8),
                   bc(Ecol.unsqueeze(2), [128, 8, 64]), ALU.mult)
                tt(t1, t1, PS[2][:, 0:512], ALU.add)
                tt(t2.rearrange("p (h q) -> p h q", h=8), xstok.rearrange("p (h q) -> p h q", h=8),
                   bc(dsk_b.unsqueeze(2), [128, 8, 64]), ALU.mult)
                tt(t1, t1, t2, ALU.add)
                tt(t1, t1, zs, ALU.mult)
                yield '.'
                for g in range(2):
                    act(t2[:, g * 256:(g + 1) * 256], t1[:, g * 256:(g + 1) * 256], AF.Square, accum_out=ssq2[:, g:g + 1])
                act(rs2, ssq2, AF.Ln, bias=epsc, scale=1.0 / 256.0)
                act(rs2, rs2, AF.Exp, scale=-0.5)
                yield '.'
                for g in range(2):
                    stt(mtok[:, g * 256:(g + 1) * 256], t1[:, g * 256:(g + 1) * 256], rs2[:, g:g + 1],
                        gssd_b[:, g * 256:(g + 1) * 256], ALU.mult, ALU.mult)

                yield '.'
                if not is_s:
                    for g in range(2):
                        mm(PS[4][:, g * 256:(g + 1) * 256], Btok[:, g * 128:(g + 1) * 128], xdtw[:, g * 256:(g + 1) * 256])
                    yield '.'
                    tt(ST.rearrange("p (h q) -> p h q", h=8), ST.rearrange("p (h q) -> p h q", h=8),
                       bc(DEC.unsqueeze(2), [128, 8, 64]), ALU.mult)
                    tt(ST, ST, PS[4][:, 0:512], ALU.add)
                    yield '.'
                    cp(STb, ST, eng="act")
                    if last_p:
                        for hp in range(4):
                            tr(PS[4][:, hp * 128:(hp + 1) * 128], ST[:, hp * 128:(hp + 1) * 128], ident)
                        cp(ostage.rearrange("p q n -> p (q n)"), PS[4][:, 0:512], eng="act")
                        osem = P.dsem()
                        store_sems.append(osem)
                        dma("sp", ossd_p_d.rearrange("(q t) p n -> (t p) q n", t=2), ostage, osem)

                yield 'S2'
                if not is_s:
                    for h in range(4):
                        mm(PS[3][:, h * 128:(h + 1) * 128], attT[:, h, :], vtok[:, h * 128:(h + 1) * 128], start=True, stop=False)
                        mm(PS[3][:, h * 128:(h + 1) * 128], qtil[0:64, h, :], SGb[0:64, h, :], start=False, stop=True)
                    yield '.'
                else:
                    M.reset(mk_s)
                    sg1 = [M.f32(4, 128) for _ in range(2)]
                    sg1_sem = [P.dsem() for _ in range(2)]
                    sg1b = SGb
                    ostT = SG.rearrange("p a b -> p (a b)")
                    kdm = [M.bf16(256) for _ in range(2)]
                    decG = M.f32(4, 16)
                    sgn = [M.f32(4, 128) for _ in range(2)]
                    sgn_sem = [P.dsem() for _ in range(2)]
                    store_sems.extend(sgn_sem)
                    for h in range(4):
                        mm(PS[1][0:64, h * 16:(h + 1) * 16], loga[:, h * 64:(h + 1) * 64], SEQ)
                    act(decG[0:64].rearrange("p a b -> p (a b)"), PS[1][0:64, 0:64], AF.Exp)
                    for s_ in range(16):
                        b = s_ % 2
                        dma("pool", sg1[b][0:64], sgla_d[s_].rearrange("h d v -> d h v"), sg1_sem[b])
                        cp(sg1b[0:64].rearrange("p a b -> p (a b)"), sg1[b][0:64].rearrange("p a b -> p (a b)"), eng="act")
                        for h in range(4):
                            mm(PS[0][:, h * 128 + 8 * s_: h * 128 + 8 * s_ + 8], sg1b[0:64, h, :], qtil[0:64, h, 8 * s_:8 * s_ + 8])
                        ts(kdm[b], kd, SEQ[:, s_:s_ + 1], None, ALU.mult)
                        for h in range(4):
                            mm(PS[7][0:64, h * 128:(h + 1) * 128], kdm[b][:, h * 64:(h + 1) * 64], vtok[:, h * 128:(h + 1) * 128])
                        for h in range(4):
                            stt(sgn[b][0:64, h, :], sg1[b][0:64, h, :], decG[0:64, h, s_:s_ + 1], PS[7][0:64, h * 128:(h + 1) * 128],
                                ALU.mult, ALU.add)
                        dma("sp", ogla_s_d[s_].rearrange("h d v -> d h v"), sgn[b][0:64], sgn_sem[b])
                    cp(ostT, PS[0][:, 0:512], eng="act")
                    for h in range(4):
                        mm(PS[3][:, h * 128:(h + 1) * 128], attT[:, h, :], vtok[:, h * 128:(h + 1) * 128], start=True, stop=False)
                        mm(PS[3][:, h * 128:(h + 1) * 128], ostT[:, h * 128:(h + 1) * 128], ident, start=False, stop=True)
                if not is_s:
                    for h in range(4):
                        mm(PS[7][0:64, h * 128:(h + 1) * 128], kd[:, h * 64:(h + 1) * 64], vtok[:, h * 128:(h + 1) * 128])
                    yield '.'
                    for h in range(4):
                        stt(SG[0:64, h, :], SG[0:64, h, :], eq[0:64, h, 127:128], PS[7][0:64, h * 128:(h + 1) * 128], ALU.mult, ALU.add)
                    cp(SGb[0:64].rearrange("p a b -> p (a b)"), SG[0:64].rearrange("p a b -> p (a b)"), eng="act")
                    if last_p:
                        gsem = P.dsem()
                        store_sems.append(gsem)
                        dma("sp", ogla_p_d.rearrange("h d v -> d h v"), SG[0:64], gsem)
                yield 'S3a'
                for h in range(4):
                    act(acc[0], PS[3][:, h * 128:(h + 1) * 128], AF.Square, accum_out=ssq4[:, h:h + 1])
                act(rs4, ssq4, AF.Ln, bias=epsc, scale=1.0 / 128.0)
                act(rs4, rs4, AF.Exp, scale=-0.5)
                yield '.'
                tt(gs.rearrange("p (h v) -> p h v", h=4), gs.rearrange("p (h v) -> p h v", h=4),
                   bc(ggla_b.unsqueeze(1), [128, 4, 128]), ALU.mult)
                for h in range(4):
                    stt(mtok[:, 512 + h * 128:512 + (h + 1) * 128], PS[3][:, h * 128:(h + 1) * 128], rs4[:, h:h + 1],
                        gs[:, h * 128:(h + 1) * 128], ALU.mult, ALU.mult)

                yield '.'
                if last_p or is_s:
                    tm_proj(PS[1][:, 0:512], 512, 512)
                    tm_proj(PS[2][:, 0:512], 1024, 512)
                    cp(cvo[:, 0:512], PS[1][:, 0:512], eng="act")
                    cp(cvo[:, 512:1024], PS[2][:, 0:512], eng="act")
                    csem = P.dsem()
                    store_sems.append(csem)
                    if last_p:
                        dma("sp", oconv_p_d, cvo[125:128, :], csem)
                    else:
                        for s_ in range(16):
                            dma("sp", oconv_s_d[s_], cvo[8 * s_ + 5:8 * s_ + 8, :], csem)

                par = 0 if is_s else (c % 2)
                mT = mT2[:, par]
                for kc in range(KC):
                    tr(PS2b[:, kc * 128:(kc + 1) * 128], mtok[:, kc * 128:(kc + 1) * 128], identb)
                cp(mT.rearrange("p a b -> p (a b)"), PS2b[:, 0:1024], eng="act")
                yield '.'
                if is_s:
                    for dc in range(KC):
                        po = PS[3 if dc % 2 == 0 else 4][:, 0:128]
                        for kc in range(KC):
                            mm(po, wout[:, kc, dc * 128:(dc + 1) * 128], mT[:, kc, :], start=(kc == 0), stop=(kc == KC - 1))
                        tt(xT[:, dc, tok], xT[:, dc, tok], po, ALU.add)
                        yield '.'
                elif par == 1:
                    tok2 = slice(tok0 - 128, tok0 + 128)
                    for dc in range(KC):
                        pb = PS[3 if dc % 2 == 0 else 4]
                        po3 = pb[:, 0:256].rearrange("p (a b) -> p a b", a=2)
                        for kc in range(KC):
                            mm(po3, wout[:, kc, dc * 128:(dc + 1) * 128], mT2[:, :, kc, :], start=(kc == 0), stop=(kc == KC - 1))
                        tt(xT[:, dc, tok2], xT[:, dc, tok2], pb[:, 0:256], ALU.add)
                        yield '.'
                yield 'S3'

            def run_all(g):
                for _ in g:
                    pass

            mk_sets = M.mark()
            if not DBG_SKIP_SAMPLE:
                run_all(chunk_gen(NPC, setA))
            M.reset(mk_sets)
            setB = alloc_set()
            memset(ST, 0.0)
            memset(STb, 0.0)
            memset(SG, 0.0)
            memset(SGb, 0.0)
            memset(pad, 0.0)
            sets = [setA, setB]
            gens = [chunk_gen(c, sets[c % 2]) for c in range(NPC)]

            def run_to(g, end):
                while True:
                    r = next(g)
                    if r == end:
                        return
                    assert r == '.', (r, end)

            def interleave(A, endA, B, endB, passB=(), ratio=1, passA=()):
                a_done = b_done = False
                while not (a_done and b_done):
                    if not b_done:
                        r = next(B)
                        if r == endB:
                            b_done = True
                        else:
                            assert r == '.' or r in passB, (r, endB)
                    for _ in range(ratio):
                        if not a_done:
                            r = next(A)
                            if r == endA:
                                a_done = True
                            else:
                                assert r == '.' or r in passA, (r, endA)

            run_to(gens[0], 'P1')
            run_to(gens[0], 'P2')
            run_to(gens[0], 'S1')
            for c in range(NPC):
                if c + 1 < NPC:
                    if FINE_INTERLEAVE:
                        interleave(gens[c + 1], 'P2', gens[c], 'S3a', passB=('S2',), ratio=IL_RATIO, passA=('P1',))
                        interleave(gens[c + 1], 'S1', gens[c], 'S3')
                    else:
                        run_to(gens[c + 1], 'P1')
                        run_to(gens[c + 1], 'P2')
                        run_to(gens[c], 'S2')
                        run_to(gens[c], 'S3a')
                        run_to(gens[c + 1], 'S1')
                        run_to(gens[c], 'S3')
                else:
                    run_to(gens[c], 'S2')
                    run_to(gens[c], 'S3a')
                    run_to(gens[c], 'S3')
            M.reset(mk)

        if do_mixer:
            mixer()

        TGF = 512
        sq_f = M.bf16(2, TGF)
        rstd_f = M.f32(TGF)
        yb = [M.f32(TGF) for _ in range(4)]
        ysem = [P.dsem() for _ in range(4)]
        store_sems += ysem
        yi = [0]

        def final_tg(t0, n):
            rms_rstd(t0, n, sq_f, rstd_f, PS[7][:, 0:n])
            for kc in range(KC):
                b = yi[0] % 4
                yi[0] += 1
                stt(yb[b][:, 0:n], xT[:, kc, t0:t0 + n], pv[:, PV_GF + kc: PV_GF + kc + 1], rstd_f[:, 0:n], ALU.mult, ALU.mult)
                dma("sp", yT_d[kc * 128:(kc + 1) * 128, t0:t0 + n], yb[b][:, 0:n], ysem[b])

        if do_ffn2:
            ffn(w2i_d, w2o_d, PV_G2, "f2", on_done=final_tg)
        else:
            for t0 in range(0, NT, TGF):
                final_tg(t0, min(TGF, NT - t0))

        P.wait_all("sp", store_sems)

        P.finalize_counts()
        for d in P.dsems:
            d.h = es.enter_context(nc.semaphore("d%d" % d.idx))
        esem = {}
        for e in Prog.ENGS:
            esem[e] = [es.enter_context(nc.semaphore("e_%s_%d" % (e, i))) for i in range(P.nsem[e])]
        with nc.Block() as block:
            @block.tensor
            def _(e):
                P.emit_engine("pe", e, esem)

            @block.scalar
            def _(e):
                P.emit_engine("act", e, esem)

            @block.vector
            def _(e):
                P.emit_engine("dve", e, esem)

            @block.gpsimd
            def _(e):
                P.emit_engine("pool", e, esem)

            @block.sync
            def _(e):
                P.emit_engine("sp", e, esem)
    stats = {k: len(v) for k, v in P.ops.items()}
    stats["sbuf_top"] = M.top
    stats["n_dsem"] = len(P.dsems)
    stats["n_esem"] = sum(P.nsem.values())
    return nc, stats


def prep_core_inputs(inp, core, PSEQ=2048):
    xp = np.asarray(inp["x_prompt"][core, :PSEQ], np.float32)
    xs = np.asarray(inp["x_sample"][16 * core:16 * core + 16], np.float32).reshape(128, D)
    x = np.concatenate([xp, xs], axis=0)
    return {
        "xT": np.ascontiguousarray(x.T),
        "st_ssd": np.ascontiguousarray(inp["state_ssd"][0, 16 * core:16 * core + 16]),
        "st_conv": np.ascontiguousarray(inp["state_conv"][0, 16 * core:16 * core + 16]),
        "st_gla": np.ascontiguousarray(inp["state_gla"][0, 16 * core:16 * core + 16]),
    }


def shared_inputs(inp):
    return {
        "w1i": np.ascontiguousarray(inp["w_ffn1_in"][0], np.float32),
        "w1o": np.ascontiguousarray(inp["w_ffn1_out"][0], np.float32),
        "wi": np.ascontiguousarray(inp["w_in"][0], np.float32),
        "wo": np.ascontiguousarray(inp["w_out"][0], np.float32),
        "w2i": np.ascontiguousarray(inp["w_ffn2_in"][0], np.float32),
        "w2o": np.ascontiguousarray(inp["w_ffn2_out"][0], np.float32),
        "consts": make_consts(),
        "pvec": make_pvec(inp),
        "wgk2": np.ascontiguousarray(inp["w_gk2"][0], np.float32),
    }


def run(inp, PSEQ=2048, ncores=NCORES, trace=False, **flags):
    nc, stats = build(PSEQ=PSEQ, **flags)
    sh = shared_inputs(inp)
    in_maps = []
    for c in range(ncores):
        m = dict(sh)
        m.update(prep_core_inputs(inp, c, PSEQ))
        in_maps.append(m)
    res = run_bass_kernel_spmd(nc, in_maps, core_ids=list(range(ncores)), trace=trace)
    return res, stats


def assemble(res, PSEQ=2048, ncores=NCORES):
    R = res.results
    yp = np.stack([np.asarray(R[c]["yT"])[:, :PSEQ].T for c in range(ncores)])
    ys = np.concatenate([np.asarray(R[c]["yT"])[:, PSEQ:].T.reshape(16, 8, D) for c in range(ncores)])
    ssd_p = np.stack([np.asarray(R[c]["o_ssd_p"]) for c in range(ncores)])[None]
    conv_p = np.stack([np.asarray(R[c]["o_conv_p"]) for c in range(ncores)])[None]
    gla_p = np.stack([np.asarray(R[c]["o_gla_p"]) for c in range(ncores)])[None]
    ssd_s = np.concatenate([np.asarray(R[c]["o_ssd_s"]) for c in range(ncores)])[None]
    conv_s = np.concatenate([np.asarray(R[c]["o_conv_s"]) for c in range(ncores)])[None]
    gla_s = np.concatenate([np.asarray(R[c]["o_gla_s"]) for c in range(ncores)])[None]
    return tuple(np.ascontiguousarray(a, dtype=np.float32) for a in
                 (yp, ys, ssd_p, conv_p, gla_p, ssd_s, conv_s, gla_s))


def kernel(**inputs):
    res, _ = run(inputs)
    return assemble(res)
```
